# Optimizing a Trainium2 kernel written in Bass

```python
import math
import jax, jax.numpy as jnp
from jax import lax
import numpy as np

D_MODEL = 1024
BATCH = 8
SEQ = 4096
DEPTH = 2

N_MIXERS = 2
D_FF = 256 * (-(-(8 * D_MODEL) // (3 * 256)))
ROPE_THETA = 10000.0
EPS = 1e-6
NEG_INF = -1e30
A_HEAD_DIM = 64
A_HEADS = D_MODEL // (2 * A_HEAD_DIM)
A_Q_BLOCK = 128
B_HEAD_DIM = 64
B_HEADS = D_MODEL // B_HEAD_DIM
B_GROUPS = ((128, 1), (512, 4), (2048, 16))
B_BLOCK = 64
N_A_LAYERS = (DEPTH + 1) // 2
N_B_LAYERS = DEPTH // 2

kernel_name = 'hybrid_diffattn_dilated_macaron_encoder'


def rms_norm(x, g):
    xf = x.astype(jnp.float32)
    y = xf * lax.rsqrt(jnp.mean(xf * xf, axis=-1, keepdims=True) + EPS)
    return (y * g.astype(jnp.float32)).astype(x.dtype)


def rope_tables(seq, dim):
    inv = ROPE_THETA ** (-jnp.arange(0, dim, 2, dtype=jnp.float32) / dim)
    ang = jnp.arange(seq, dtype=jnp.float32)[:, None] * inv[None, :]
    return jnp.cos(ang), jnp.sin(ang)


def apply_rope(x, cos, sin):
    shape = (cos.shape[0],) + (1,) * (x.ndim - 3) + (cos.shape[1],)
    c, s = cos.reshape(shape), sin.reshape(shape)
    xf = x.astype(jnp.float32)
    x1, x2 = jnp.split(xf, 2, axis=-1)
    return jnp.concatenate([x1 * c - x2 * s, x2 * c + x1 * s], axis=-1).astype(x.dtype)


def swiglu(h, w_gate, w_up, w_down):
    return (jax.nn.silu(h @ w_gate) * (h @ w_up)) @ w_down


def diff_attention(h, w_qkv, w_o, lam, subln, lambda_init, cos, sin):
    B_, S_, _ = h.shape
    q, k, v = jnp.split(h @ w_qkv, 3, axis=-1)
    q = q.reshape(B_, S_, A_HEADS, 2, A_HEAD_DIM)
    k = k.reshape(B_, S_, A_HEADS, 2, A_HEAD_DIM)
    v = v.reshape(B_, S_, A_HEADS, 2 * A_HEAD_DIM)
    q = apply_rope(q, cos, sin) * (A_HEAD_DIM ** -0.5)
    k = apply_rope(k, cos, sin)
    lamf = lam.astype(jnp.float32)
    lam_full = jnp.exp(jnp.sum(lamf[0] * lamf[1])) - jnp.exp(jnp.sum(lamf[2] * lamf[3])) + lambda_init
    nq = S_ // A_Q_BLOCK
    qb = q.reshape(B_, nq, A_Q_BLOCK, A_HEADS, 2, A_HEAD_DIM).transpose(1, 0, 2, 3, 4, 5)

    def block(qblk):
        s = jnp.einsum('bqhcd,bkhcd->bhcqk', qblk, k, preferred_element_type=jnp.float32)
        p = jax.nn.softmax(s, axis=-1)
        a = p[:, :, 0] - lam_full * p[:, :, 1]
        return jnp.einsum('bhqk,bkhe->bqhe', a.astype(v.dtype), v)

    o = lax.map(block, qb)
    o = o.transpose(1, 0, 2, 3, 4).reshape(B_, S_, A_HEADS, 2 * A_HEAD_DIM)
    o = rms_norm(o, subln) * (1.0 - lambda_init)
    return o.reshape(B_, S_, D_MODEL) @ w_o


def dilated_group_attention(q, k, v, dilation, half):
    B_, S_, H_, hd = q.shape
    L = S_ // dilation
    nb = -(-L // B_BLOCK)
    Lp = nb * B_BLOCK

    def to_strided(t):
        t = t.reshape(B_, L, dilation, H_, t.shape[-1]).transpose(0, 2, 3, 1, 4)
        return jnp.pad(t, ((0, 0), (0, 0), (0, 0), (0, Lp - L), (0, 0)))

    def band(t):
        tp = jnp.pad(t, ((0, 0), (0, 0), (0, 0), (B_BLOCK, B_BLOCK), (0, 0)))
        tb = tp.reshape(B_, dilation, H_, nb + 2, B_BLOCK, t.shape[-1])
        return jnp.concatenate([tb[:, :, :, :-2], tb[:, :, :, 1:-1], tb[:, :, :, 2:]], axis=4)

    qb = to_strided(q).reshape(B_, dilation, H_, nb, B_BLOCK, hd)
    kb = band(to_strided(k))
    vb = band(to_strided(v))
    s = jnp.einsum('bphnqe,bphnke->bphnqk', qb, kb, preferred_element_type=jnp.float32)
    qi = jnp.arange(Lp).reshape(nb, B_BLOCK, 1)
    kj = (jnp.arange(nb)[:, None] * B_BLOCK - B_BLOCK + jnp.arange(3 * B_BLOCK)[None, :]).reshape(nb, 1, 3 * B_BLOCK)
    valid = (jnp.abs(qi - kj) <= half) & (kj >= 0) & (kj < L)
    s = jnp.where(valid, s, NEG_INF)
    m = jnp.max(s, axis=-1, keepdims=True)
    p = jnp.exp(s - m)
    den = jnp.sum(p, axis=-1, keepdims=True)
    o = jnp.einsum('bphnqk,bphnke->bphnqe', (p / den).astype(v.dtype), vb)
    lse = m + jnp.log(den)

    def from_strided(t):
        t = t[:, :, :, :L]
        return t.transpose(0, 3, 1, 2, 4).reshape(B_, S_, H_, t.shape[-1])

    o = from_strided(o.reshape(B_, dilation, H_, Lp, hd))
    lse = from_strided(lse.reshape(B_, dilation, H_, Lp, 1))[..., 0]
    return o, lse


def dilated_mixture_attention(h, w_in, w_o, cos, sin):
    B_, S_, _ = h.shape
    n_groups = len(B_GROUPS)
    parts = jnp.split(h @ w_in, 2 * n_groups + 1, axis=-1)
    v = parts[-1].reshape(B_, S_, B_HEADS, B_HEAD_DIM)
    outs, lses = [], []
    for g, (window, dilation) in enumerate(B_GROUPS):
        q = apply_rope(parts[2 * g].reshape(B_, S_, B_HEADS, B_HEAD_DIM), cos, sin) * (B_HEAD_DIM ** -0.5)
        k = apply_rope(parts[2 * g + 1].reshape(B_, S_, B_HEADS, B_HEAD_DIM), cos, sin)
        o, lse = dilated_group_attention(q, k, v, dilation, window // (2 * dilation))
        outs.append(o)
        lses.append(lse)
    alpha = jax.nn.softmax(jnp.stack(lses, axis=0), axis=0)
    o = jnp.sum(jnp.stack(outs, axis=0).astype(jnp.float32) * alpha[..., None], axis=0).astype(h.dtype)
    return o.reshape(B_, S_, D_MODEL) @ w_o


def setup_inputs(seed: int = 0) -> dict:
    key = jax.random.key(seed)
    ks = jax.random.split(key, 20)
    n_in = 2 * len(B_GROUPS) + 1
    f32 = jnp.float32

    def w(k, shape, fan_in):
        return jax.random.normal(k, shape, f32) * (fan_in ** -0.5)

    def gain(k, shape):
        return 1.0 + 0.01 * jax.random.normal(k, shape, f32)

    return {
        'x': jax.random.normal(ks[0], (BATCH, SEQ, D_MODEL), f32),
        'ln_ffn1': gain(ks[1], (DEPTH, D_MODEL)),
        'w1_gate': w(ks[2], (DEPTH, D_MODEL, D_FF), D_MODEL),
        'w1_up': w(ks[3], (DEPTH, D_MODEL, D_FF), D_MODEL),
        'w1_down': w(ks[4], (DEPTH, D_FF, D_MODEL), D_FF),
        'ln_mix': gain(ks[5], (DEPTH, D_MODEL)),
        'a_w_qkv': w(ks[6], (N_A_LAYERS, D_MODEL, 3 * D_MODEL), D_MODEL),
        'a_w_o': w(ks[7], (N_A_LAYERS, D_MODEL, D_MODEL), D_MODEL),
        'a_lambda': 0.1 * jax.random.normal(ks[8], (N_A_LAYERS, 4, A_HEAD_DIM), f32),
        'a_subln': gain(ks[9], (N_A_LAYERS, 2 * A_HEAD_DIM)),
        'b_w_in': w(ks[10], (N_B_LAYERS, D_MODEL, n_in * D_MODEL), D_MODEL),
        'b_w_o': w(ks[11], (N_B_LAYERS, D_MODEL, D_MODEL), D_MODEL),
        'ln_ffn2': gain(ks[12], (DEPTH, D_MODEL)),
        'w2_gate': w(ks[13], (DEPTH, D_MODEL, D_FF), D_MODEL),
        'w2_up': w(ks[14], (DEPTH, D_MODEL, D_FF), D_MODEL),
        'w2_down': w(ks[15], (DEPTH, D_FF, D_MODEL), D_FF),
        'ln_final': gain(ks[16], (D_MODEL,)),
    }


def reference(x, ln_ffn1, w1_gate, w1_up, w1_down, ln_mix, a_w_qkv, a_w_o, a_lambda, a_subln,
              b_w_in, b_w_o, ln_ffn2, w2_gate, w2_up, w2_down, ln_final):
    cos, sin = rope_tables(x.shape[1], A_HEAD_DIM)
    for i in range(DEPTH):
        x = x + 0.5 * swiglu(rms_norm(x, ln_ffn1[i]), w1_gate[i], w1_up[i], w1_down[i])
        h = rms_norm(x, ln_mix[i])
        j = i // N_MIXERS
        if i % N_MIXERS == 0:
            lambda_init = 0.8 - 0.6 * math.exp(-0.3 * i)
            x = x + diff_attention(h, a_w_qkv[j], a_w_o[j], a_lambda[j], a_subln[j], lambda_init, cos, sin)
        else:
            x = x + dilated_mixture_attention(h, b_w_in[j], b_w_o[j], cos, sin)
        x = x + 0.5 * swiglu(rms_norm(x, ln_ffn2[i]), w2_gate[i], w2_up[i], w2_down[i])
    return rms_norm(x, ln_final)
```

```python
import numpy as np
from contextlib import ExitStack
import concourse.bass as bass
import concourse.mybir as mybir
from concourse.bass_utils import run_bass_kernel_spmd

F32 = mybir.dt.float32
BF16 = mybir.dt.bfloat16
AF = mybir.ActivationFunctionType
ALU = mybir.AluOpType

D = 1024
S = 4096
DFF = 2816
NFC = DFF // 128
T = 512
NT = S // T
EPS = 1e-6
NCORES = 8

ENGS = ("pe", "act", "dve", "pool", "sp")
NDMASEM = 8


class Buf:
    __slots__ = ("name", "w", "r")

    def __init__(self, name):
        self.name = name
        self.w = None
        self.r = []


class Op:
    __slots__ = ("eng", "fn", "deps", "idx", "sig", "semval", "dma", "dsem", "dval", "waits", "epoch")

    def __init__(self, eng, fn, dma):
        self.eng = eng
        self.fn = fn
        self.dma = dma
        self.deps = []
        self.sig = False
        self.semval = None
        self.dsem = None
        self.dval = None
        self.waits = []
        self.epoch = 0


class Prog:
    def __init__(self, nc, stack):
        self.nc = nc
        self.ops = {e: [] for e in ENGS}
        self.nops = {e: 0 for e in ENGS}
        self.sem = {e: stack.enter_context(nc.semaphore("s_" + e)) for e in ENGS}
        self.semcnt = {e: 0 for e in ENGS}
        self.dsems, self.dtot, self.drr, self.dlast = {}, {}, {}, {}
        for q in ("sp", "pool", "act"):
            self.dsems[q] = [stack.enter_context(nc.semaphore("d_%s%d" % (q, i))) for i in range(NDMASEM)]
            self.dtot[q] = [0] * NDMASEM
            self.dlast[q] = [None] * NDMASEM
            self.drr[q] = 0
        self.waited = {e: {} for e in ENGS}
        self.lastc = {e: None for e in ENGS}
        self.epoch = 0
        self.pre_barrier = None

    def op(self, eng, fn, reads=(), writes=(), dma=False):
        o = Op(eng, fn, dma)
        o.idx = self.nops[eng]
        self.nops[eng] += 1
        o.epoch = self.epoch
        deps = []
        for b in reads:
            if b.w is not None:
                deps.append(b.w)
        for b in writes:
            if b.w is not None:
                deps.append(b.w)
            deps.extend(b.r)
        for b in reads:
            b.r.append(o)
        for b in writes:
            b.w = o
            b.r = []
        if dma:
            q = eng
            i = self.drr[q]
            self.drr[q] = (i + 1) % NDMASEM
            o.dsem = self.dsems[q][i]
            prev = self.dlast[q][i]
            if prev is not None:
                deps.append(prev)
            self.dtot[q][i] += 16
            o.dval = self.dtot[q][i]
            self.dlast[q][i] = o
        else:
            self.lastc[eng] = o
        o.deps = deps
        self.ops[eng].append(o)
        return o

    def barrier(self):
        if self.pre_barrier is not None:
            self.pre_barrier()
        lasts = [self.lastc[e] for e in ENGS if self.lastc[e] is not None]
        dmas = [o for q in self.dlast for o in self.dlast[q] if o is not None]
        for e in ENGS:
            o = Op(e, lambda eng: eng.nop(), False)
            o.idx = self.nops[e]
            self.nops[e] += 1
            o.epoch = self.epoch
            o.deps = [x for x in lasts if x.eng != e] + dmas
            self.ops[e].append(o)
        self.epoch += 1

    def flush(self):
        nc = self.nc
        for e in ENGS:
            for o in self.ops[e]:
                need = {}
                dm = []
                for d in o.deps:
                    if d.dma:
                        dm.append(d)
                        continue
                    if d.epoch != o.epoch:
                        continue
                    if d.eng == o.eng and (o.eng == "pe" or (o.idx - d.idx) > 2):
                        continue
                    if d.eng not in need or need[d.eng].idx < d.idx:
                        need[d.eng] = d
                o.deps = list(need.values()) + dm
                for d in need.values():
                    d.sig = True
        for e in ENGS:
            for o in self.ops[e]:
                if o.sig and not o.dma and o.semval is None:
                    self.semcnt[e] += 1
                    o.semval = self.semcnt[e]
        for e in ENGS:
            wd = self.waited[e]
            for o in self.ops[e]:
                need = {}
                for d in o.deps:
                    if d.dma:
                        s, v = d.dsem, d.dval
                    else:
                        assert d.semval is not None
                        s, v = self.sem[d.eng], d.semval
                    k = id(s)
                    if k not in need or need[k][1] < v:
                        need[k] = (s, v)
                for k, (s, v) in need.items():
                    if wd.get(k, 0) >= v:
                        continue
                    wd[k] = v
                    o.waits.append((s, v))
        engmap = {"pe": "tensor", "act": "scalar", "dve": "vector", "pool": "gpsimd", "sp": "sync"}
        with nc.Block() as block:
            for e in ENGS:
                ops = self.ops[e]
                if not ops:
                    continue

                def body(eng, ops=ops, e=e):
                    for o in ops:
                        for (s, v) in o.waits:
                            eng.wait_ge(s, v)
                        ins = o.fn(eng)
                        if o.dma:
                            ins.then_inc(o.dsem, 16)
                        elif o.sig:
                            ins.then_inc(self.sem[e], 1)

                getattr(block, engmap[e])(body)
        for e in ENGS:
            self.ops[e] = []


class Ctx:
    pass


def declare_io(nc, C):
    def din(name, shape, dt=F32):
        return nc.dram_tensor(name, list(shape), dt, kind="ExternalInput").ap()

    C.x = din("x", (S, D))
    C.ln_ffn1 = din("ln_ffn1", (2, D))
    C.w1_gate = din("w1_gate", (2, D, DFF))
    C.w1_up = din("w1_up", (2, D, DFF))
    C.w1_down = din("w1_down", (2, DFF, D))
    C.ln_mix = din("ln_mix", (2, D))
    C.a_w_qkv = din("a_w_qkv", (1, D, 3 * D))
    C.a_w_o = din("a_w_o", (1, D, D))
    C.a_lambda = din("a_lambda", (1, 4, 64))
    C.a_subln = din("a_subln", (1, 128))
    C.b_w_in = din("b_w_in", (1, D, 7 * D))
    C.b_w_o = din("b_w_o", (1, D, D))
    C.ln_ffn2 = din("ln_ffn2", (2, D))
    C.w2_gate = din("w2_gate", (2, D, DFF))
    C.w2_up = din("w2_up", (2, D, DFF))
    C.w2_down = din("w2_down", (2, DFF, D))
    C.ln_final = din("ln_final", (1, D))
    C.ident = din("c_ident", (128, 128), BF16)
    C.cs = din("c_cs", (S, 128))
    C.mask = din("c_mask", (128, 512), BF16)
    C.out = nc.dram_tensor("out", [S, D], F32, kind="ExternalOutput").ap()
    C.xs = nc.dram_tensor("xs", [S, D], F32).ap()


class Alloc:
    _n = [0]

    def __init__(self, nc, es):
        self.nc, self.es = nc, es
        Alloc._n[0] += 1
        self.pfx = "p%d_" % Alloc._n[0]

    def sb(self, name, shape, dt):
        return self.es.enter_context(self.nc.sbuf_tensor(self.pfx + name, list(shape), dt))

    def ps(self, name, shape, dt):
        return self.es.enter_context(self.nc.psum_tensor(self.pfx + name, list(shape), dt))


class NormT:
    def __init__(self, P, nc, C, A, gain_row, x_src, XSRC, transpose=True):
        self.P, self.C, self.x_src, self.XSRC = P, C, x_src, XSRC
        self.xt = [A.sb("xt%d" % i, (128, 4, D), F32) for i in range(2)]
        self.XT = [Buf("xt%d" % i) for i in range(2)]
        self.junk = A.sb("junk", (128, D), BF16)
        self.ss = A.sb("ss", (128, 8), F32)
        self.vv = A.sb("vv", (128, 8), F32)
        self.rstd = A.sb("rstd", (128, 8), F32)
        self.SS = [Buf("ss%d" % i) for i in range(2)]
        self.VV = [Buf("vv%d" % i) for i in range(2)]
        self.RS = [Buf("rs%d" % i) for i in range(2)]
        self.mhalf = A.sb("mhalf", (128, 4), F32)
        self.MH = Buf("mhalf")
        self.gbc = A.sb("gbc", (128, D), F32)
        self.GB = Buf("gbc")
        self.transpose = transpose
        if transpose:
            self.ident = A.sb("ident", (128, 128), BF16)
            self.ID = Buf("ident")
            self.xn2 = [A.sb("xn%d" % i, (128, 4, D), BF16) for i in range(2)]
            self.XN2 = [[Buf("xn%d_%d" % (i, j)) for j in range(4)] for i in range(2)]
            self.xnT = [A.sb("xnT%d" % i, (128, 8, T), BF16) for i in range(2)]
            self.XNT = [Buf("xnT%d" % i) for i in range(2)]
            self.pt = [A.ps("pt%d" % i, (128, D), BF16) for i in range(2)]
            self.PT = [Buf("pt%d" % i) for i in range(2)]
            P.op("sp", lambda e: e.dma_start(out=self.ident[:], in_=C.ident), writes=(self.ID,), dma=True)
        P.op("sp", lambda e: e.dma_start(out=self.gbc[:], in_=gain_row.partition_broadcast(128)), writes=(self.GB,), dma=True)
        P.op("pool", lambda e: e.memset(self.mhalf[:], -0.5), writes=(self.MH,))
        self.cnt = 0

    def load(self, i, b):
        P = self.P
        xt, XT = self.xt, self.XT
        src = self.x_src[i * T:(i + 1) * T, :].rearrange("(j p) d -> p j d", p=128)
        P.op("sp", lambda e: e.dma_start(out=xt[b][:], in_=src), reads=(self.XSRC[i],), writes=(XT[b],), dma=True)

    def stats(self, b):
        P = self.P
        xt, XT, ss, vv, rstd = self.xt, self.XT, self.ss, self.vv, self.rstd
        for j in range(4):
            P.op("act", lambda e, j=j: e.activation(out=self.junk[:], in_=xt[b][:, j, :], func=AF.Square,
                                                    accum_out=ss[:, 4 * b + j:4 * b + j + 1]),
                 reads=(XT[b],), writes=(self.SS[b],))
        P.op("dve", lambda e: e.tensor_scalar(out=vv[:, 4 * b:4 * b + 4], in0=ss[:, 4 * b:4 * b + 4],
                                              scalar1=1.0 / D, scalar2=EPS, op0=ALU.mult, op1=ALU.add),
             reads=(self.SS[b],), writes=(self.VV[b],))
        P.op("pool", lambda e: e.tensor_tensor(out=rstd[:, 4 * b:4 * b + 4], in0=vv[:, 4 * b:4 * b + 4],
                                               in1=self.mhalf[:], op=ALU.pow),
             reads=(self.VV[b], self.MH), writes=(self.RS[b],))

    def prep(self, i, b):
        P = self.P
        self.load(i, b)
        self.stats(b)
        xt, XT, rstd = self.xt, self.XT, self.rstd
        for j in range(4):
            P.op("dve", lambda e, j=j: e.scalar_tensor_tensor(out=self.xn2[b][:, j, :], in0=xt[b][:, j, :],
                                                              scalar=rstd[:, 4 * b + j:4 * b + j + 1], in1=self.gbc[:],
                                                              op0=ALU.mult, op1=ALU.mult),
                 reads=(XT[b], self.RS[b], self.GB), writes=(self.XN2[b][j],))

    def trans(self, b):
        P = self.P
        for j in range(4):
            q = self.cnt % 2
            self.cnt += 1
            for k in range(8):
                P.op("pe", lambda e, j=j, k=k, q=q: e.transpose(out=self.pt[q][:, k * 128:(k + 1) * 128],
                                                                in_=self.xn2[b][:, j, k * 128:(k + 1) * 128],
                                                                identity=self.ident[:]),
                     reads=(self.XN2[b][j], self.ID), writes=(self.PT[q],))
            src = self.pt[q][:].rearrange("p (k t) -> p k t", k=8)
            dst = self.xnT[b][:, :, j * 128:(j + 1) * 128]
            if self.cnt % 2 == 0:
                P.op("act", lambda e, src=src, dst=dst: e.copy(out=dst, in_=src), reads=(self.PT[q],), writes=(self.XNT[b],))
            else:
                P.op("dve", lambda e, src=src, dst=dst: e.tensor_copy(out=dst, in_=src), reads=(self.PT[q],), writes=(self.XNT[b],))

    def load_norm(self, i, b):
        self.prep(i, b)
        self.trans(b)


def ffn_phase(P, nc, C, x_src, XSRC, x_dst, XDST, gain_row, wg, wu, wd, final_gain=None, bg=None):
    with ExitStack() as es:
        A = Alloc(nc, es)
        N = NormT(P, nc, C, A, gain_row, x_src, XSRC)
        xt, XT, xnT, XNT = N.xt, N.XT, N.xnT, N.XNT
        NW = 3
        wgt = [A.sb("wgt%d" % i, (128, 8, 256), BF16) for i in range(NW)]
        wut = [A.sb("wut%d" % i, (128, 8, 256), BF16) for i in range(NW)]
        WG = [Buf("wg%d" % i) for i in range(NW)]
        WU = [Buf("wu%d" % i) for i in range(NW)]
        wdt = A.sb("wdt", (128, NFC, D), BF16)
        WD = Buf("wdt")
        hT = A.sb("hT", (128, NFC, T), BF16)
        HT = [Buf("hT%d" % c) for c in range(NFC)]
        sg = [A.sb("sg%d" % i, (128, T), F32) for i in range(2)]
        SG = [Buf("sg%d" % i) for i in range(2)]
        pg = [A.ps("pg%d" % i, (128, T), F32) for i in range(2)]
        PG = [Buf("pg%d" % i) for i in range(2)]
        pu = [A.ps("pu%d" % i, (128, T), F32) for i in range(2)]
        PU = [Buf("pu%d" % i) for i in range(2)]
        pd = [A.ps("pd%d" % i, (128, T), F32) for i in range(2)]
        PD = [Buf("pd%d" % i) for i in range(2)]
        if final_gain is not None:
            fgb = A.sb("fgb", (128, D), F32)
            FG = Buf("fgb")
            fss = A.sb("fss", (128, 8), F32)
            fvv = A.sb("fvv", (128, 8), F32)
            frs = A.sb("frs", (128, 8), F32)
            FSS = [Buf("fss%d" % i) for i in range(2)]
            FVV = [Buf("fvv%d" % i) for i in range(2)]
            FRS = [Buf("frs%d" % i) for i in range(2)]
            P.op("sp", lambda e: e.dma_start(out=fgb[:], in_=final_gain.partition_broadcast(128)), writes=(FG,), dma=True)

        wd_v = wd.rearrange("(c p) d -> p c d", p=128)
        half = NFC // 2
        P.op("sp", lambda e: e.dma_start(out=wdt[:, 0:half, :], in_=wd_v[:, 0:half, :]), writes=(WD,), dma=True)
        P.op("sp", lambda e: e.dma_start(out=wdt[:, half:NFC, :], in_=wd_v[:, half:NFC, :]), writes=(WD,), dma=True)
        wg_v = wg.rearrange("(k p) f -> p k f", p=128)
        wu_v = wu.rearrange("(k p) f -> p k f", p=128)
        cnt = {"w": 0, "pp": 0, "pd": 0}

        def gate_up(i, b):
            for grp in range(NFC // 2):
                if grp == 3 and i + 1 < NT:
                    N.prep(i + 1, 1 - b)
                s = cnt["w"] % NW
                cnt["w"] += 1
                c0 = grp * 256
                P.op("sp", lambda e, s=s, c0=c0: e.dma_start(out=wgt[s][:], in_=wg_v[:, :, c0:c0 + 256]),
                     writes=(WG[s],), dma=True)
                P.op("sp", lambda e, s=s, c0=c0: e.dma_start(out=wut[s][:], in_=wu_v[:, :, c0:c0 + 256]),
                     writes=(WU[s],), dma=True)
                for c in range(2):
                    fc = grp * 2 + c
                    q = cnt["pp"] % 2
                    cnt["pp"] += 1
                    for k in range(8):
                        P.op("pe", lambda e, s=s, c=c, k=k, q=q: e.matmul(pg[q][:], lhsT=wgt[s][:, k, c * 128:(c + 1) * 128],
                                                                          rhs=xnT[b][:, k, :], start=(k == 0), stop=(k == 7)),
                             reads=(WG[s], XNT[b]), writes=(PG[q],))
                    for k in range(8):
                        P.op("pe", lambda e, s=s, c=c, k=k, q=q: e.matmul(pu[q][:], lhsT=wut[s][:, k, c * 128:(c + 1) * 128],
                                                                          rhs=xnT[b][:, k, :], start=(k == 0), stop=(k == 7)),
                             reads=(WU[s], XNT[b]), writes=(PU[q],))
                    P.op("act", lambda e, q=q: e.activation(out=sg[q][:], in_=pg[q][:], func=AF.Silu),
                         reads=(PG[q],), writes=(SG[q],))
                    P.op("dve", lambda e, q=q, fc=fc: e.tensor_tensor(out=hT[:, fc, :], in0=sg[q][:], in1=pu[q][:], op=ALU.mult),
                         reads=(SG[q], PU[q]), writes=(HT[fc],))

        def down_res(i, b):
            for j in range(4):
                for h in range(2):
                    q = cnt["pd"] % 2
                    cnt["pd"] += 1
                    for fc in range(NFC):
                        P.op("pe", lambda e, j=j, h=h, fc=fc, q=q: e.matmul(pd[q][:], lhsT=hT[:, fc, j * 128:(j + 1) * 128],
                                                                            rhs=wdt[:, fc, h * 512:(h + 1) * 512],
                                                                            start=(fc == 0), stop=(fc == NFC - 1)),
                             reads=(HT[fc], WD), writes=(PD[q],))
                    P.op("dve", lambda e, j=j, h=h, q=q: e.scalar_tensor_tensor(
                        out=xt[b][:, j, h * 512:(h + 1) * 512], in0=pd[q][:], scalar=0.5,
                        in1=xt[b][:, j, h * 512:(h + 1) * 512], op0=ALU.mult, op1=ALU.add),
                         reads=(PD[q], XT[b]), writes=(XT[b],))
            if final_gain is not None:
                for j in range(4):
                    P.op("act", lambda e, j=j: e.activation(out=N.junk[:], in_=xt[b][:, j, :], func=AF.Square,
                                                            accum_out=fss[:, 4 * b + j:4 * b + j + 1]),
                         reads=(XT[b],), writes=(FSS[b],))
                P.op("dve", lambda e: e.tensor_scalar(out=fvv[:, 4 * b:4 * b + 4], in0=fss[:, 4 * b:4 * b + 4],
                                                      scalar1=1.0 / D, scalar2=EPS, op0=ALU.mult, op1=ALU.add),
                     reads=(FSS[b],), writes=(FVV[b],))
                P.op("pool", lambda e: e.tensor_tensor(out=frs[:, 4 * b:4 * b + 4], in0=fvv[:, 4 * b:4 * b + 4],
                                                       in1=N.mhalf[:], op=ALU.pow),
                     reads=(FVV[b], N.MH), writes=(FRS[b],))
                for j in range(4):
                    P.op("dve", lambda e, j=j: e.scalar_tensor_tensor(out=xt[b][:, j, :], in0=xt[b][:, j, :],
                                                                      scalar=frs[:, 4 * b + j:4 * b + j + 1], in1=fgb[:],
                                                                      op0=ALU.mult, op1=ALU.mult),
                         reads=(XT[b], FRS[b], FG), writes=(XT[b],))
            dst = x_dst[i * T:(i + 1) * T, :].rearrange("(j p) d -> p j d", p=128)
            P.op("pool", lambda e: e.dma_start(out=dst, in_=xt[b][:]), reads=(XT[b],), writes=(XDST[i],), dma=True)
            if bg is not None:
                bg.tick(2)

        N.load_norm(0, 0)
        for i in range(NT):
            b = i % 2
            gate_up(i, b)
            if i + 1 < NT:
                N.trans(1 - b)
            down_res(i, b)
        P.barrier()
        P.flush()


def qkv_phase(P, nc, C, x_src, XSRC, gain_row, roped, vspec, bg=None):
    with ExitStack() as es:
        A = Alloc(nc, es)
        N = NormT(P, nc, C, A, gain_row, x_src, XSRC)
        xnT, XNT = N.xnT, N.XNT
        NWB = 3
        wt = [A.sb("wt%d" % i, (128, 8, D), BF16) for i in range(NWB)]
        WT = [Buf("wt%d" % i) for i in range(NWB)]
        cs = [A.sb("cs%d" % i, (128, 4, 128), F32) for i in range(3)]
        CS = [Buf("cs%d" % i) for i in range(3)]
        xr = [A.sb("xr%d" % i, (128, D), F32) for i in range(3)]
        XR = [Buf("xr%d" % i) for i in range(3)]
        t1 = [A.sb("t1_%d" % i, (128, D), F32) for i in range(2)]
        T1 = [Buf("t1_%d" % i) for i in range(2)]
        t2 = [A.sb("t2_%d" % i, (128, D), F32) for i in range(2)]
        T2 = [Buf("t2_%d" % i) for i in range(2)]
        ro = [A.sb("ro%d" % i, (128, D), BF16) for i in range(2)]
        RO = [Buf("ro%d" % i) for i in range(2)]
        stg = [A.sb("stg%d" % i, (128, 8, T), BF16) for i in range(2)]
        STG = [Buf("stg%d" % i) for i in range(2)]
        vst = [A.sb("vst%d" % i, (128, 4, D), BF16) for i in range(2)]
        VST = [Buf("vst%d" % i) for i in range(2)]
        pr = [A.ps("pr%d" % i, (128, D), F32) for i in range(2)]
        PR = [Buf("pr%d" % i) for i in range(2)]
        pq = [A.ps("pq%d" % i, (128, D), BF16) for i in range(2)]
        PQ = [Buf("pq%d" % i) for i in range(2)]
        (vw, vdst, VDST, vmode) = vspec
        mats = [(w, dst, DST, "r") for (w, dst, DST) in roped] + [(vw, vdst, VDST, "v")]
        NM = len(mats)
        items = []
        for i in range(NT):
            for mi, (w, dst, DST, kind) in enumerate(mats):
                for j in range(4):
                    items.append({"i": i, "mi": mi, "j": j, "kind": kind, "g": i * NM + mi})
        NI = len(items)

        def load_w(gidx):
            i, mi = divmod(gidx, NM)
            if i >= NT:
                return
            s = gidx % NWB
            wv = mats[mi][0].rearrange("(k p) f -> p k f", p=128)
            P.op("sp", lambda e: e.dma_start(out=wt[s][:, 0:4, :], in_=wv[:, 0:4, :]), writes=(WT[s],), dma=True)
            P.op("sp", lambda e: e.dma_start(out=wt[s][:, 4:8, :], in_=wv[:, 4:8, :]), writes=(WT[s],), dma=True)

        def load_cs(i):
            if i >= NT:
                return
            cb = i % 3
            csrc = C.cs[i * T:(i + 1) * T, :].rearrange("(j p) c -> p j c", p=128)
            P.op("sp", lambda e: e.dma_start(out=cs[cb][:], in_=csrc), writes=(CS[cb],), dma=True)

        def st_a(n):
            it = items[n]
            i, j, s, b = it["i"], it["j"], it["g"] % NWB, it["i"] % 2
            if it["j"] == 0:
                if it["mi"] == 0:
                    if i + 1 < NT:
                        N.prep(i + 1, 1 - b)
                    load_cs(i + 1)
                    if bg is not None:
                        bg.tick(2)
                if it["mi"] == NM - 1 and i + 1 < NT:
                    N.trans(1 - b)
                load_w(it["g"] + 1)
            q = n % 2
            for h in range(2):
                for k in range(8):
                    P.op("pe", lambda e, h=h, k=k: e.matmul(pr[q][:, h * 512:(h + 1) * 512], lhsT=xnT[b][:, k, j * 128:(j + 1) * 128],
                                                            rhs=wt[s][:, k, h * 512:(h + 1) * 512], start=(k == 0), stop=(k == 7)),
                         reads=(XNT[b], WT[s]), writes=(PR[q],))

        def st_b(n):
            it = items[n]
            q = n % 2
            if it["kind"] == "v":
                vs_ = it["i"] % 2
                j, i = it["j"], it["i"]
                P.op("act", lambda e: e.copy(out=vst[vs_][:, j, :], in_=pr[q][:]), reads=(PR[q],), writes=(VST[vs_],))
                if j == 3:
                    t0 = i * T
                    if vmode == "nat":
                        dv = vdst[t0:t0 + T, :].rearrange("(j p) d -> p j d", p=128)
                        P.op("pool", lambda e: e.dma_start(out=dv, in_=vst[vs_][:]), reads=(VST[vs_],), writes=(VDST,), dma=True)
                    else:
                        for jj in range(4):
                            dv = vdst.rearrange("h p c e -> p c h e")[:, 4 * i + jj, :, :]
                            P.op("pool", lambda e, dv=dv, jj=jj: e.dma_start(
                                out=dv, in_=vst[vs_][:, jj, :].rearrange("p (h e) -> p h e", h=8)), reads=(VST[vs_],), writes=(VDST,), dma=True)
                return
            u = n % 3
            P.op("act", lambda e: e.copy(out=xr[u][:], in_=pr[q][:]), reads=(PR[q],), writes=(XR[u],))

        def st_c(n):
            it = items[n]
            if it["kind"] == "v":
                return
            u, v, cb, j = n % 3, n % 2, it["i"] % 3, it["j"]
            cbc = cs[cb][:, j, 0:64].unsqueeze(1).to_broadcast([128, 16, 64])
            P.op("dve", lambda e: e.tensor_tensor(out=t1[v][:].rearrange("p (a c) -> p a c", a=16),
                                                  in0=xr[u][:].rearrange("p (a c) -> p a c", a=16), in1=cbc, op=ALU.mult),
                 reads=(XR[u], CS[cb]), writes=(T1[v],))
            xv = xr[u][:].rearrange("p (a z c) -> p a z c", a=16, z=2)
            tv = t2[v][:].rearrange("p (a z c) -> p a z c", a=16, z=2)
            s0 = cs[cb][:, j, 64:96].unsqueeze(1).to_broadcast([128, 16, 32])
            s1 = cs[cb][:, j, 96:128].unsqueeze(1).to_broadcast([128, 16, 32])
            P.op("pool", lambda e: e.tensor_tensor(out=tv[:, :, 0, :], in0=xv[:, :, 1, :], in1=s0, op=ALU.mult),
                 reads=(XR[u], CS[cb]), writes=(T2[v],))
            P.op("pool", lambda e: e.tensor_tensor(out=tv[:, :, 1, :], in0=xv[:, :, 0, :], in1=s1, op=ALU.mult),
                 reads=(XR[u], CS[cb]), writes=(T2[v],))

        def st_d(n):
            it = items[n]
            if it["kind"] == "v":
                return
            v = n % 2
            P.op("dve", lambda e: e.tensor_tensor(out=ro[v][:], in0=t1[v][:], in1=t2[v][:], op=ALU.add),
                 reads=(T1[v], T2[v]), writes=(RO[v],))

        def st_e(n):
            it = items[n]
            if it["kind"] == "v":
                return
            v = n % 2
            for k in range(8):
                P.op("pe", lambda e, k=k: e.transpose(out=pq[v][:, k * 128:(k + 1) * 128], in_=ro[v][:, k * 128:(k + 1) * 128],
                                                      identity=N.ident[:]), reads=(RO[v], N.ID), writes=(PQ[v],))

        def st_f(n):
            it = items[n]
            if it["kind"] == "v":
                return
            v, j, i = n % 2, it["j"], it["i"]
            sg_ = it["g"] % 2
            P.op("act", lambda e: e.copy(out=stg[sg_][:, :, j * 128:(j + 1) * 128], in_=pq[v][:].rearrange("p (k t) -> p k t", k=8)),
                 reads=(PQ[v],), writes=(STG[sg_],))
            if j == 3:
                (w, dst, DST, kind) = mats[it["mi"]]
                dv = dst.rearrange("h p t -> p h t")[:, :, i * T:(i + 1) * T]
                P.op("pool", lambda e: e.dma_start(out=dv, in_=stg[sg_][:]), reads=(STG[sg_],), writes=(DST,), dma=True)

        N.load_norm(0, 0)
        load_cs(0)
        load_w(0)
        stages = [st_a, st_b, st_c, st_d, st_e, st_f]
        for it_ in range(NI + len(stages) - 1):
            for si, st in enumerate(stages):
                n = it_ - si
                if 0 <= n < NI:
                    st(n)
        P.barrier()
        P.flush()


def att0_phase(P, nc, C, qT, kT, Vs, attnT, ATT, lam_src, lambda_init, bg=None):
    NQ = S // 512
    NK = S // 128
    with ExitStack() as es:
        A = Alloc(nc, es)
        qh = [A.sb("qh%d" % i, (128, S), BF16) for i in range(2)]
        QH = [Buf("qh%d" % i) for i in range(2)]
        kp = [[A.sb("kp%d_%d" % (i, c), (128, S), BF16) for c in range(1)] for i in range(2)]
        KP = [[Buf("kp%d_%d" % (i, c)) for c in range(1)] for i in range(2)]
        vh = [A.sb("vh%d" % i, (128, NK, 128), BF16) for i in range(2)]
        VH = [Buf("vh%d" % i) for i in range(2)]
        NPB = 3
        ptb = [A.sb("ptb%d" % i, (128, 1024), BF16) for i in range(NPB)]
        PTB = [Buf("ptb%d" % i) for i in range(NPB)]
        ones = A.sb("onesf", (128, 128), F32)
        ON = Buf("onesf")
        ad = [A.sb("ad%d" % i, (128, 1024), F32) for i in range(2)]
        AD = [[Buf("ad%d_%d" % (i, c)) for c in range(2)] for i in range(2)]
        oc = A.sb("oc", (128, 1024), F32)
        OC = Buf("oc")
        dc = A.sb("dc", (128, 1024), F32)
        DC = Buf("dc")
        rc = A.sb("rc", (128, 1024), F32)
        RC = Buf("rc")
        ta = A.sb("ta", (128, 512), F32)
        TA = Buf("ta")
        tb = A.sb("tb", (128, 512), F32)
        TB = Buf("tb")
        ob = [A.sb("ob%d" % i, (128, 512), BF16) for i in range(2)]
        OB = [Buf("ob%d" % i) for i in range(2)]
        lamb = A.sb("lamb", (128, 256), F32)
        lprod = A.sb("lprod", (128, 128), F32)
        lsum = A.sb("lsum", (128, 2), F32)
        lex = A.sb("lex", (128, 2), F32)
        nl = A.sb("nl", (128, 1), F32)
        LB, LP, LS, LE, NL = Buf("lamb"), Buf("lprod"), Buf("lsum"), Buf("lex"), Buf("nl")
        sp_ = [A.ps("sp%d" % i, (128, 1024), F32) for i in range(2)]
        SP_ = [Buf("sp%d" % i) for i in range(2)]
        po = A.ps("po", (128, 1024), F32)
        PO = Buf("po")
        pn = A.ps("pn", (128, 1024), F32)
        PN = Buf("pn")

        P.op("dve", lambda e: e.memset(ones[:], 1.0), writes=(ON,))
        lsrc = lam_src.rearrange("a b -> (a b)").unsqueeze(0)
        P.op("sp", lambda e: e.dma_start(out=lamb[:], in_=lsrc.partition_broadcast(128)), writes=(LB,), dma=True)
        lv = lamb[:].rearrange("p (a b c) -> p a b c", a=2, b=2)
        P.op("dve", lambda e: e.tensor_tensor(out=lprod[:].rearrange("p (a c) -> p a c", a=2), in0=lv[:, :, 0, :], in1=lv[:, :, 1, :], op=ALU.mult),
             reads=(LB,), writes=(LP,))
        P.op("dve", lambda e: e.reduce_sum(out=lsum[:], in_=lprod[:].rearrange("p (a c) -> p a c", a=2), axis=mybir.AxisListType.X),
             reads=(LP,), writes=(LS,))
        P.op("act", lambda e: e.activation(out=lex[:], in_=lsum[:], func=AF.Exp), reads=(LS,), writes=(LE,))
        P.op("dve", lambda e: e.scalar_tensor_tensor(out=nl[:], in0=lex[:, 1:2], scalar=-float(lambda_init), in1=lex[:, 0:1],
                                                     op0=ALU.add, op1=ALU.subtract), reads=(LE,), writes=(NL,))
        cnt = {"s": 0, "p": 0, "ob": 0}

        def load_head(h, hb):
            P.op("sp", lambda e: e.dma_start(out=qh[hb][:], in_=qT[h]), writes=(QH[hb],), dma=True)
            P.op("sp", lambda e: e.dma_start(out=kp[hb][0][:], in_=kT[h]), writes=(KP[hb][0],), dma=True)
            P.op("sp", lambda e: e.dma_start(out=vh[hb][:], in_=Vs[h]), writes=(VH[hb],), dma=True)

        def s_mm(hb, qt, kc):
            sb_ = cnt["s"] % 2
            cnt["s"] += 1
            for c in range(2):
                P.op("pe", lambda e, c=c, sb_=sb_: e.matmul(sp_[sb_][:, c * 512:(c + 1) * 512],
                                                            lhsT=kp[hb][0][c * 64:(c + 1) * 64, kc * 128:(kc + 1) * 128],
                                                            rhs=qh[hb][c * 64:(c + 1) * 64, qt * 512:(qt + 1) * 512], start=True, stop=True),
                     reads=(KP[hb][0], QH[hb]), writes=(SP_[sb_],))
            return sb_

        DSPLIT = 768

        def exp_pv(hb, kc, sb_, x):
            pb = cnt["p"] % NPB
            cnt["p"] += 1
            P.op("act", lambda e: e.activation(out=ptb[pb][:], in_=sp_[sb_][:], func=AF.Exp, scale=0.125),
                 reads=(SP_[sb_],), writes=(PTB[pb],))
            for c in range(2):
                P.op("pe", lambda e, c=c: e.matmul(po[:, c * 512:(c + 1) * 512], lhsT=vh[hb][:, kc, :],
                                                   rhs=ptb[pb][:, c * 512:(c + 1) * 512], start=(kc == 0), stop=(kc == NK - 1)),
                     reads=(VH[hb], PTB[pb]), writes=(PO,))
            for (eng, lo, hi, ci) in (("dve", 0, DSPLIT, 0), ("pool", DSPLIT, 1024, 1)):
                if kc == 0:
                    P.op(eng, lambda e, lo=lo, hi=hi: e.tensor_copy(out=ad[x][:, lo:hi], in_=ptb[pb][:, lo:hi]),
                         reads=(PTB[pb],), writes=(AD[x][ci],))
                else:
                    P.op(eng, lambda e, lo=lo, hi=hi: e.tensor_tensor(out=ad[x][:, lo:hi], in0=ad[x][:, lo:hi], in1=ptb[pb][:, lo:hi], op=ALU.add),
                         reads=(PTB[pb], AD[x][ci]), writes=(AD[x][ci],))

        def epi1(x):
            for c in range(2):
                P.op("pe", lambda e, c=c: e.matmul(pn[:, c * 512:(c + 1) * 512], lhsT=ones[:], rhs=ad[x][:, c * 512:(c + 1) * 512],
                                                   start=True, stop=True), reads=(ON, AD[x][0], AD[x][1]), writes=(PN,))
            P.op("dve", lambda e: e.tensor_copy(out=oc[:], in_=po[:]), reads=(PO,), writes=(OC,))

        def epi2(h, qt):
            P.op("act", lambda e: e.activation(out=rc[:], in_=pn[:], func=AF.Ln), reads=(PN,), writes=(RC,))
            P.op("act", lambda e: e.activation(out=rc[:], in_=rc[:], func=AF.Exp, scale=-1.0), reads=(RC,), writes=(RC,))
            P.op("dve", lambda e: e.tensor_tensor(out=ta[:], in0=oc[:, 0:512], in1=rc[:, 0:512], op=ALU.mult), reads=(OC, RC), writes=(TA,))
            P.op("dve", lambda e: e.tensor_tensor(out=tb[:], in0=oc[:, 512:1024], in1=rc[:, 512:1024], op=ALU.mult), reads=(OC, RC), writes=(TB,))
            o_ = cnt["ob"] % 2
            cnt["ob"] += 1
            P.op("dve", lambda e: e.scalar_tensor_tensor(out=ob[o_][:], in0=tb[:], scalar=nl[:, 0:1], in1=ta[:], op0=ALU.mult, op1=ALU.add),
                 reads=(TA, TB, NL), writes=(OB[o_],))
            P.op("pool", lambda e: e.dma_start(out=attnT[h][:, qt * 512:(qt + 1) * 512], in_=ob[o_][:]), reads=(OB[o_],), writes=(ATT,), dma=True)
            if bg is not None:
                bg.tick(1)

        load_head(0, 0)
        deferred = []
        nu = 0
        for h in range(8):
            hb = h % 2
            if h + 1 < 8:
                load_head(h + 1, 1 - hb)
            for qt in range(NQ):
                x = nu % 2
                nu += 1
                sb_ = s_mm(hb, qt, 0)
                for kc in range(NK):
                    nxt = s_mm(hb, qt, kc + 1) if kc + 1 < NK else None
                    exp_pv(hb, kc, sb_, x)
                    sb_ = nxt
                    if kc == 3 and deferred:
                        deferred.pop(0)()
                epi1(x)
                deferred.append(lambda h=h, qt=qt: epi2(h, qt))
        while deferred:
            deferred.pop(0)()
        P.barrier()
        P.flush()


def oproj_phase(P, nc, C, x, XB, attnT, wo, subln=None, bg=None):
    with ExitStack() as es:
        A = Alloc(nc, es)
        xt = [A.sb("xt%d" % i, (128, 4, D), F32) for i in range(2)]
        XT = [Buf("xt%d" % i) for i in range(2)]
        at = [A.sb("at%d" % i, (128, 8, T), BF16) for i in range(2)]
        AT = [Buf("at%d" % i) for i in range(2)]
        wot = A.sb("wot", (128, 8, D), BF16)
        WO = Buf("wot")
        po_ = [A.ps("po%d" % i, (128, 512), F32) for i in range(2)]
        PO_ = [Buf("po%d" % i) for i in range(2)]
        wv = wo.rearrange("(k p) f -> p k f", p=128)
        P.op("sp", lambda e: e.dma_start(out=wot[:, 0:4, :], in_=wv[:, 0:4, :]), writes=(WO,), dma=True)
        P.op("sp", lambda e: e.dma_start(out=wot[:, 4:8, :], in_=wv[:, 4:8, :]), writes=(WO,), dma=True)
        if subln is not None:
            (sub_row, scale) = subln
            sq = [A.sb("sq%d" % i, (128, 512), BF16) for i in range(2)]
            SQ = [Buf("sq%d" % i) for i in range(2)]
            v_ = [A.sb("v%d" % i, (128, 512), F32) for i in range(2)]
            V_ = [Buf("v%d" % i) for i in range(2)]
            r_ = [A.sb("r%d" % i, (128, 512), F32) for i in range(2)]
            R_ = [Buf("r%d" % i) for i in range(2)]
            on = [A.sb("on%d" % i, (128, 8, T), BF16) for i in range(2)]
            ONB = [[Buf("on%d_%d" % (i, h)) for h in range(8)] for i in range(2)]
            gcol = A.sb("gcol", (128, 1), F32)
            gcs = A.sb("gcs", (128, 1), F32)
            GC, GCS = Buf("gcol"), Buf("gcs")
            ones = A.sb("ones", (128, 128), BF16)
            ONE = Buf("ones")
            pm = [A.ps("pm%d" % i, (128, 512), F32) for i in range(2)]
            PM = [Buf("pm%d" % i) for i in range(2)]
            P.op("sp", lambda e: e.dma_start(out=gcol[:], in_=sub_row.rearrange("a e -> e a")), writes=(GC,), dma=True)
            P.op("dve", lambda e: e.tensor_scalar(out=gcs[:], in0=gcol[:], scalar1=float(scale), scalar2=None, op0=ALU.mult), reads=(GC,), writes=(GCS,))
            epst = A.sb("epst", (128, 1), F32)
            EPT = Buf("epst")
            P.op("pool", lambda e: e.memset(epst[:], EPS), writes=(EPT,))
            P.op("dve", lambda e: e.memset(ones[:], 1.0), writes=(ONE,))
        cnt = {"q": 0, "m": 0}

        def load(i, b):
            t0 = i * T
            P.op("sp", lambda e: e.dma_start(out=xt[b][:], in_=x[t0:t0 + T, :].rearrange("(j p) d -> p j d", p=128)),
                 reads=(XB[i],), writes=(XT[b],), dma=True)
            P.op("sp", lambda e: e.dma_start(out=at[b][:], in_=attnT.rearrange("h p t -> p h t")[:, :, t0:t0 + T]),
                 writes=(AT[b],), dma=True)

        def proj(i, b, src, SRC):
            t0 = i * T
            for j in range(4):
                for hf in range(2):
                    q = cnt["q"] % 2
                    cnt["q"] += 1
                    for h in range(8):
                        P.op("pe", lambda e, j=j, hf=hf, h=h, q=q: e.matmul(po_[q][:], lhsT=src[:, h, j * 128:(j + 1) * 128],
                                                                            rhs=wot[:, h, hf * 512:(hf + 1) * 512],
                                                                            start=(h == 0), stop=(h == 7)),
                             reads=(SRC[h], WO), writes=(PO_[q],))
                    P.op("dve", lambda e, j=j, hf=hf, q=q: e.tensor_tensor(out=xt[b][:, j, hf * 512:(hf + 1) * 512], in0=po_[q][:],
                                                                           in1=xt[b][:, j, hf * 512:(hf + 1) * 512], op=ALU.add),
                         reads=(PO_[q], XT[b]), writes=(XT[b],))
            P.op("pool", lambda e: e.dma_start(out=x[t0:t0 + T, :].rearrange("(j p) d -> p j d", p=128), in_=xt[b][:]),
                 reads=(XT[b],), writes=(XB[i],), dma=True)
            if bg is not None:
                bg.tick(1)

        if subln is None:
            load(0, 0)
            for i in range(NT):
                b = i % 2
                if i + 1 < NT:
                    load(i + 1, 1 - b)
                proj(i, b, at[b], [AT[b]] * 8)
        else:
            items = [(i, h) for i in range(NT) for h in range(8)]
            NI = len(items)

            def s0(n):
                i, h = items[n]
                b, m = i % 2, n % 2
                P.op("dve", lambda e: e.tensor_tensor(out=sq[m][:], in0=at[b][:, h, :], in1=at[b][:, h, :], op=ALU.mult),
                     reads=(AT[b],), writes=(SQ[m],))

            def s1(n):
                m = n % 2
                P.op("pe", lambda e: e.matmul(pm[m][:], lhsT=ones[:], rhs=sq[m][:], start=True, stop=True),
                     reads=(ONE, SQ[m]), writes=(PM[m],))

            def s2(n):
                m = n % 2
                P.op("act", lambda e: e.activation(out=v_[m][:], in_=pm[m][:], func=AF.Ln, scale=1.0 / 128, bias=epst[:, 0:1]),
                     reads=(PM[m], EPT), writes=(V_[m],))
                P.op("act", lambda e: e.activation(out=r_[m][:], in_=v_[m][:], func=AF.Exp, scale=-0.5),
                     reads=(V_[m],), writes=(R_[m],))

            def s3(n):
                i, h = items[n]
                b, m = i % 2, n % 2
                P.op("dve", lambda e: e.scalar_tensor_tensor(out=on[b][:, h, :], in0=at[b][:, h, :], scalar=gcs[:, 0:1],
                                                             in1=r_[m][:], op0=ALU.mult, op1=ALU.mult),
                     reads=(AT[b], GCS, R_[m]), writes=(ONB[b][h],))
                if h == 7:
                    proj(i, b, on[b], ONB[b])
                    if i + 2 < NT:
                        load(i + 2, b)

            load(0, 0)
            load(1, 1)
            stages = [s0, s1, s2, s3]
            for it_ in range(NI + len(stages) - 1):
                for si, st in enumerate(stages):
                    n = it_ - si
                    if 0 <= n < NI:
                        st(n)
        P.barrier()
        P.flush()


def att1_phase(P, nc, C, qk, vn1, attnT, ATT):
    DIL = (1, 4, 16)
    with ExitStack() as es:
        A = Alloc(nc, es)
        acc2 = [A.sb("acc%d" % i, (128, 2, S), F32) for i in range(2)]
        ACC2 = [Buf("acc%d" % i) for i in range(2)]
        qg = [A.sb("qg%d" % i, (128, S), BF16) for i in range(2)]
        QG = [Buf("qg%d" % i) for i in range(2)]
        kp = [[A.sb("kp%d_%d" % (i, c), (128, S), BF16) for c in range(2)] for i in range(2)]
        KP = [[Buf("kp%d_%d" % (i, c)) for c in range(2)] for i in range(2)]
        vg = [A.sb("vg%d" % i, (128, 32, 128), BF16) for i in range(3)]
        VG = [Buf("vg%d" % i) for i in range(3)]
        vp = [[A.sb("vp%d_%d" % (i, c), (128, 32, 128), BF16) for c in range(2)] for i in range(2)]
        VP = [[Buf("vp%d_%d" % (i, c)) for c in range(2)] for i in range(2)]
        NPB = 3
        ptb = [A.sb("ptb%d" % i, (128, 2, 512), BF16) for i in range(NPB)]
        PTB = [[Buf("ptb%d_%d" % (i, c)) for c in range(2)] for i in range(NPB)]
        mask = A.sb("mask", (128, 512), BF16)
        MK = Buf("mask")
        onesp = [A.sb("onesp%d" % c, (128, 128), BF16) for c in range(2)]
        ONP = [Buf("onesp%d" % c) for c in range(2)]
        lnd = A.sb("lnd", (128, S), F32)
        LND = Buf("lnd")
        outb = A.sb("outb", (128, S), BF16)
        OUTB = Buf("outb")
        sps = [A.ps("sps%d" % i, (128, 2, 512), F32) for i in range(2)]
        SPS = [Buf("sps%d" % i) for i in range(2)]
        pz = [A.ps("pz%d" % i, (128, 2, 512), F32) for i in range(2)]
        PZ = [Buf("pz%d" % i) for i in range(2)]

        identb = A.sb("identb", (128, 128), BF16)
        IDB = Buf("identb")
        P.op("sp", lambda e: e.dma_start(out=mask[:], in_=C.mask), writes=(MK,), dma=True)
        P.op("sp", lambda e: e.dma_start(out=identb[:], in_=C.ident), writes=(IDB,), dma=True)
        for c in range(2):
            P.op("pool", lambda e, c=c: e.memset(onesp[c][:], 0.0), writes=(ONP[c],))
            P.op("pool", lambda e, c=c: e.memset(onesp[c][:, c * 64:(c + 1) * 64], 1.0), writes=(ONP[c],))
        for i in range(2):
            P.op("pool", lambda e, i=i: e.memset(kp[i][0][64:128, :], 0.0), writes=(KP[i][0],))
            P.op("pool", lambda e, i=i: e.memset(kp[i][1][0:64, :], 0.0), writes=(KP[i][1],))
            P.op("pool", lambda e, i=i: e.memset(vp[i][0][:, :, 64:128], 1.0), writes=(VP[i][0],))
            P.op("pool", lambda e, i=i: e.memset(vp[i][1][:, :, 0:64], 1.0), writes=(VP[i][1],))
        cnt = {"s": 0, "p": 0, "z": 0}
        seq = [(hp, g) for hp in range(8) for g in range(3)]

        def load(hp, g, b):
            d = DIL[g]
            nj = (S // d) // 128
            q, k = qk[2 * g], qk[2 * g + 1]
            P.op("sp", lambda e: e.dma_start(out=qg[b][:], in_=q[hp]), writes=(QG[b],), dma=True)
            P.op("sp", lambda e: e.dma_start(out=kp[b][0][0:64, :], in_=k[hp][0:64, :]), writes=(KP[b][0],), dma=True)
            P.op("sp", lambda e: e.dma_start(out=kp[b][1][64:128, :], in_=k[hp][64:128, :]), writes=(KP[b][1],), dma=True)

        def load_v(n):
            hp, g = seq[n]
            d = DIL[g]
            nj = (S // d) // 128
            vb = n % 3
            vsrc = vn1.rearrange("(j a p) c -> p a j c", a=128, p=d)
            for p in range(d):
                P.op("sp", lambda e, p=p: e.dma_start(out=vg[vb][:, p * nj:(p + 1) * nj, :],
                                                      in_=vsrc[p, :, :, hp * 128:(hp + 1) * 128]),
                     writes=(VG[vb],), dma=True)

        def build_vp(n):
            b, vb = n % 2, n % 3
            for c in range(2):
                P.op("act", lambda e, c=c: e.copy(out=vp[b][c][:, :, c * 64:(c + 1) * 64], in_=vg[vb][:, :, c * 64:(c + 1) * 64]),
                     reads=(VG[vb],), writes=(VP[b][c],))

        def units_for(g):
            d = DIL[g]
            L = S // d
            nj = L // 128
            bl = []
            for p in range(d):
                us = []
                for jj in range(nj):
                    lo, hi = 128 * jj - 64, 128 * jj + 192
                    off = max(0, -lo)
                    lo, hi = max(lo, 0), min(hi, L)
                    us.append((lo, hi - lo, off, jj))
                for n in range(0, len(us), 2):
                    bl.append((p, us[n:n + 2]))
            return bl

        def s_mm(b, g, batch):
            d = DIL[g]
            p, us = batch
            sb_ = cnt["s"] % 2
            cnt["s"] += 1
            for u, (mq0, nq, off, jj) in enumerate(us):
                qs = mq0 * d + p
                ks = jj * 128 * d + p
                col = u * 256 + off
                for hh in range(2):
                    P.op("pe", lambda e, hh=hh, ks=ks, qs=qs, nq=nq, col=col: e.matmul(
                        sps[sb_][:, hh, col:col + nq], lhsT=kp[b][hh][:, ks:ks + 127 * d + 1:d],
                        rhs=qg[b][:, qs:qs + (nq - 1) * d + 1:d], start=True, stop=False),
                         reads=(KP[b][hh], QG[b]), writes=(SPS[sb_],))
                    P.op("pe", lambda e, hh=hh, nq=nq, col=col: e.matmul(
                        sps[sb_][:, hh, col:col + nq], lhsT=identb[:], rhs=mask[:, col:col + nq], start=False, stop=True),
                         reads=(IDB, MK), writes=(SPS[sb_],))
            return sb_

        pending = []

        def stage_b(sb_):
            pb = cnt["p"] % NPB
            cnt["p"] += 1
            P.op("act", lambda e: e.activation(out=ptb[pb][:], in_=sps[sb_][:], func=AF.Exp, scale=0.125),
                 reads=(SPS[sb_],), writes=(PTB[pb][0], PTB[pb][1]))
            return pb

        def stage_cd(b, g, batch, pb, zero_acc, hp):
            acc, ACC = acc2[hp % 2], ACC2[hp % 2]
            d = DIL[g]
            L = S // d
            nj = L // 128
            p, us = batch
            while pending:
                pending.pop(0)()
            if zero_acc:
                P.op("pool", lambda e: e.memset(acc[:], 0.0), writes=(ACC,))
            zb = cnt["z"] % 2
            cnt["z"] += 1
            base = 128 * us[0][3] - 64
            for u, (mq0, nq, off, jj) in enumerate(us):
                ci = p * nj + jj
                col = u * 256 + off
                oc_ = mq0 - base
                for hh in range(2):
                    P.op("pe", lambda e, hh=hh, ci=ci, col=col, nq=nq, oc_=oc_, u=u: e.matmul(
                        pz[zb][:, hh, oc_:oc_ + nq], lhsT=vp[b][hh][:, ci, :], rhs=ptb[pb][:, hh, col:col + nq],
                        start=(u == 0), stop=(u == len(us) - 1), skip_group_check=True),
                         reads=(VP[b][hh], PTB[pb][hh]), writes=(PZ[zb],))
            lo = us[0][0]
            hi = us[-1][0] + us[-1][1]
            span = hi - lo
            c0 = lo - base

            def do_acc():
                qs = lo * d + p
                av = acc[:, :, qs:qs + (span - 1) * d + 1:d]
                zv = pz[zb][:, :, c0:c0 + span]
                P.op("dve", lambda e: e.tensor_tensor(out=av, in0=av, in1=zv, op=ALU.add), reads=(PZ[zb], ACC), writes=(ACC,))
            pending.append(do_acc)

        def finish_a(hp):
            acc, ACC = acc2[hp % 2], ACC2[hp % 2]
            while pending:
                pending.pop(0)()
            P.op("sp", lambda e: e.dma_start(out=lnd[0:64, :], in_=acc[64:128, 0, :]), reads=(ACC,), writes=(LND,), dma=True)
            P.op("sp", lambda e: e.dma_start(out=lnd[64:128, :], in_=acc[0:64, 1, :]), reads=(ACC,), writes=(LND,), dma=True)

        def finish_b(hp):
            acc, ACC = acc2[hp % 2], ACC2[hp % 2]
            P.op("act", lambda e: e.activation(out=lnd[:], in_=lnd[:], func=AF.Ln), reads=(LND,), writes=(LND,))
            P.op("act", lambda e: e.activation(out=lnd[:], in_=lnd[:], func=AF.Exp, scale=-1.0), reads=(LND,), writes=(LND,))
            P.op("dve", lambda e: e.tensor_tensor(out=outb[0:64, :], in0=acc[0:64, 0, :], in1=lnd[0:64, :], op=ALU.mult), reads=(ACC, LND), writes=(OUTB,))
            P.op("dve", lambda e: e.tensor_tensor(out=outb[64:128, :], in0=acc[64:128, 1, :], in1=lnd[64:128, :], op=ALU.mult), reads=(ACC, LND), writes=(OUTB,))
            P.op("pool", lambda e: e.dma_start(out=attnT[hp], in_=outb[:]), reads=(OUTB,), writes=(ATT,), dma=True)

        flat = []
        for n, (hp, g) in enumerate(seq):
            bl = units_for(g)
            for i, batch in enumerate(bl):
                flat.append((n, hp, g, n % 2, batch, i == 0, i == len(bl) - 1, i == max(0, len(bl) - 4)))
        load(seq[0][0], seq[0][1], 0)
        load_v(0)
        load_v(1)
        build_vp(0)
        sbs = {}
        NF = len(flat)
        fin_at = {}
        for i in range(NF + 2):
            if i < NF:
                (n, hp, g, b, batch, first, lastb, late) = flat[i]
                sbs[i] = s_mm(b, g, batch)
            if 0 <= i - 1 < NF:
                sbs[i - 1] = stage_b(sbs[i - 1])
            if 0 <= i - 2 < NF:
                (n, hp, g, b, batch, first, lastb, late) = flat[i - 2]
                if first and n + 1 < len(seq):
                    load(seq[n + 1][0], seq[n + 1][1], 1 - b)
                if first and n + 2 < len(seq):
                    load_v(n + 2)
                if late and n + 1 < len(seq):
                    build_vp(n + 1)
                stage_cd(b, g, batch, sbs.pop(i - 2), first and g == 0, hp)
                if lastb and g == 2:
                    finish_a(hp)
                    fin_at[i + 8] = hp
            if i in fin_at:
                finish_b(fin_at.pop(i))
        for k in sorted(fin_at):
            finish_b(fin_at[k])
        P.barrier()
        P.flush()


def build(stop_after=None):
    nc = bass.Bass("TRN2", target_bir_lowering=False)
    C = Ctx()
    declare_io(nc, C)
    with ExitStack() as st:
        P = Prog(nc, st)
        XIN = [Buf("xin%d" % i) for i in range(NT)]
        XS = [Buf("xs%d" % i) for i in range(NT)]
        XO = [Buf("xo%d" % i) for i in range(NT)]
        wb = {}

        class BG:
            def __init__(self):
                self.todo = []

            def add(self, name, src, rows, cols, c0=0):
                dst = nc.dram_tensor("wb_" + name, [rows, cols], BF16).ap()
                wb[name] = dst
                for r0 in range(0, rows, 256):
                    r1 = min(rows, r0 + 256)
                    self.todo.append((name, lambda e, r0=r0, r1=r1, dst=dst, src=src, c0=c0, cols=cols: e.dma_start(
                        out=dst[r0:r1, :], in_=src[r0:r1, c0:c0 + cols], max_dma_last_dim=4096)))

            def tick(self, n):
                for _ in range(n):
                    if not self.todo:
                        return
                    name, fn = self.todo.pop(0)
                    P.op("pool", fn, dma=True)

            def need(self, names):
                while any(nm in names for nm, _ in self.todo):
                    self.tick(1)

        bg = BG()

        def scratch(name, shape, dt=BF16):
            if stop_after is not None and stop_after.startswith("dbg"):
                return nc.dram_tensor(name, list(shape), dt, kind="ExternalOutput").ap()
            return nc.dram_tensor(name, list(shape), dt).ap()

        bg.add("g1_0", C.w1_gate[0], D, DFF)
        bg.add("u1_0", C.w1_up[0], D, DFF)
        bg.add("d1_0", C.w1_down[0], DFF, D)
        for n in range(3):
            bg.add("a_qkv%d" % n, C.a_w_qkv[0], D, D, c0=n * D)
        bg.add("a_o", C.a_w_o[0], D, D)
        bg.add("g2_0", C.w2_gate[0], D, DFF)
        bg.add("u2_0", C.w2_up[0], D, DFF)
        bg.add("d2_0", C.w2_down[0], DFF, D)
        bg.add("g1_1", C.w1_gate[1], D, DFF)
        bg.add("u1_1", C.w1_up[1], D, DFF)
        bg.add("d1_1", C.w1_down[1], DFF, D)
        for n in range(7):
            bg.add("b_in%d" % n, C.b_w_in[0], D, D, c0=n * D)
        bg.add("b_o", C.b_w_o[0], D, D)
        bg.add("g2_1", C.w2_gate[1], D, DFF)
        bg.add("u2_1", C.w2_up[1], D, DFF)
        bg.add("d2_1", C.w2_down[1], DFF, D)
        bg.need(("g1_0", "u1_0", "d1_0"))
        P.barrier()
        P.flush()

        qk = [scratch("qk%d" % n, (8, 128, S)) for n in range(6)]
        QK = [Buf("qk%d" % n) for n in range(6)]
        vs0 = scratch("vs0", (8, 128, 32, 128))
        vn1 = scratch("vn1", (S, D))
        VS = Buf("vs")
        attnT = scratch("attnT", (8, 128, S))
        ATT = Buf("attnT")
        xs = C.xs

        def done(tag, src):
            return stop_after == tag

        last = (stop_after == "ffn1_0")
        P.pre_barrier = lambda: bg.need(("a_qkv0", "a_qkv1", "a_qkv2"))
        ffn_phase(P, nc, C, C.x, XIN, C.out if last else xs, XO if last else XS, C.ln_ffn1[0:1, :], wb["g1_0"], wb["u1_0"], wb["d1_0"], bg=bg)
        if last:
            return nc
        P.pre_barrier = None
        qkv_phase(P, nc, C, xs, XS, C.ln_mix[0:1, :], [(wb["a_qkv0"], qk[0], QK[0]), (wb["a_qkv1"], qk[1], QK[1])],
                  (wb["a_qkv2"], vs0, VS, "hpce"), bg=bg)
        if stop_after == "dbg_qkv0":
            return nc
        P.pre_barrier = lambda: bg.need(("a_o",))
        att0_phase(P, nc, C, qk[0], qk[1], vs0, attnT, ATT, C.a_lambda[0], 0.8 - 0.6 * float(np.exp(-0.3 * 0)), bg=bg)
        if stop_after == "dbg_att0":
            return nc
        P.pre_barrier = lambda: bg.need(("g2_0", "u2_0", "d2_0"))
        oproj_phase(P, nc, C, xs, XS, attnT, wb["a_o"], subln=(C.a_subln[0:1, :], 1.0 - (0.8 - 0.6 * float(np.exp(-0.3 * 0)))), bg=bg)
        if stop_after == "dbg_oproj0":
            return nc
        last = (stop_after == "layer0")
        P.pre_barrier = lambda: bg.need(("g1_1", "u1_1", "d1_1"))
        ffn_phase(P, nc, C, xs, XS, C.out if last else xs, XO if last else XS, C.ln_ffn2[0:1, :], wb["g2_0"], wb["u2_0"], wb["d2_0"], bg=bg)
        if last:
            return nc
        P.pre_barrier = lambda: bg.need(tuple("b_in%d" % n for n in range(7)))
        ffn_phase(P, nc, C, xs, XS, xs, XS, C.ln_ffn1[1:2, :], wb["g1_1"], wb["u1_1"], wb["d1_1"], bg=bg)
        P.pre_barrier = None
        qkv_phase(P, nc, C, xs, XS, C.ln_mix[1:2, :], [(wb["b_in%d" % n], qk[n], QK[n]) for n in range(6)],
                  (wb["b_in6"], vn1, VS, "nat"))
        if stop_after == "dbg_qkv1":
            return nc
        P.pre_barrier = lambda: bg.need(("b_o",))
        att1_phase(P, nc, C, qk, vn1, attnT, ATT)
        if stop_after == "dbg_att1":
            return nc
        P.pre_barrier = lambda: bg.need(("g2_1", "u2_1", "d2_1"))
        oproj_phase(P, nc, C, xs, XS, attnT, wb["b_o"], subln=None, bg=bg)
        P.pre_barrier = None
        ffn_phase(P, nc, C, xs, XS, C.out, XO, C.ln_ffn2[1:2, :], wb["g2_1"], wb["u2_1"], wb["d2_1"], final_gain=C.ln_final[0:1, :])
    return nc


_CONSTS = None


def consts():
    global _CONSTS
    if _CONSTS is None:
        import ml_dtypes
        c = {}
        c["c_ident"] = np.eye(128, dtype=np.float32).astype(ml_dtypes.bfloat16)
        inv = (10000.0 ** (-np.arange(0, 64, 2, dtype=np.float32) / 64.0)).astype(np.float32)
        ang = np.arange(S, dtype=np.float32)[:, None] * inv[None, :]
        cos, sin = np.cos(ang).astype(np.float32), np.sin(ang).astype(np.float32)
        c["c_cs"] = np.ascontiguousarray(np.concatenate([cos, cos, -sin, sin], axis=1).astype(np.float32))
        a = np.arange(128)
        ma = (a[:, None] >= a[None, :]).astype(np.float32)
        mb = (a[:, None] <= a[None, :]).astype(np.float32)
        c["c_mask"] = np.ascontiguousarray((np.concatenate([mb, ma, mb, ma], axis=1) - 1.0) * 30000.0).astype(ml_dtypes.bfloat16)
        _CONSTS = c
    return _CONSTS


def make_in_maps(inputs):
    x = np.ascontiguousarray(np.asarray(inputs["x"], dtype=np.float32))
    shared = {}
    for k, v in inputs.items():
        if k == "x":
            continue
        a = np.ascontiguousarray(np.asarray(v, dtype=np.float32))
        if k == "ln_final":
            a = a.reshape(1, D)
        shared[k] = a
    shared.update(consts())
    in_maps = []
    for c in range(x.shape[0]):
        m = dict(shared)
        m["x"] = x[c]
        in_maps.append(m)
    return in_maps


def kernel(**inputs):
    nc = build()
    in_maps = make_in_maps(inputs)
    res = run_bass_kernel_spmd(nc, in_maps, core_ids=list(range(NCORES)))
    return np.stack([np.asarray(r["out"]) for r in res.results], axis=0).astype(np.float32)
```

```python
import numpy as np
from contextlib import ExitStack
import concourse.bass as bass
import concourse.mybir as mybir
from concourse.bass_utils import run_bass_kernel_spmd

F32 = mybir.dt.float32
BF16 = mybir.dt.bfloat16
AF = mybir.ActivationFunctionType
ALU = mybir.AluOpType

D = 1024
S = 4096
DFF = 2816
NFC = DFF // 128
T = 512
NT = S // T
EPS = 1e-6
NCORES = 8

ENGS = ("pe", "act", "dve", "pool", "sp")
NDMASEM = 8


class Buf:
    __slots__ = ("name", "w", "r")

    def __init__(self, name):
        self.name = name
        self.w = None
        self.r = []


class Op:
    __slots__ = ("eng", "fn", "deps", "idx", "sig", "semval", "dma", "dsem", "dval", "waits", "epoch")

    def __init__(self, eng, fn, dma):
        self.eng = eng
        self.fn = fn
        self.dma = dma
        self.deps = []
        self.sig = False
        self.semval = None
        self.dsem = None
        self.dval = None
        self.waits = []
        self.epoch = 0


class Prog:
    def __init__(self, nc, stack):
        self.nc = nc
        self.ops = {e: [] for e in ENGS}
        self.nops = {e: 0 for e in ENGS}
        self.sem = {e: stack.enter_context(nc.semaphore("s_" + e)) for e in ENGS}
        self.semcnt = {e: 0 for e in ENGS}
        self.dsems, self.dtot, self.drr, self.dlast = {}, {}, {}, {}
        for q in ("sp", "pool", "act"):
            self.dsems[q] = [stack.enter_context(nc.semaphore("d_%s%d" % (q, i))) for i in range(NDMASEM)]
            self.dtot[q] = [0] * NDMASEM
            self.dlast[q] = [None] * NDMASEM
            self.drr[q] = 0
        self.waited = {e: {} for e in ENGS}
        self.lastc = {e: None for e in ENGS}
        self.epoch = 0
        self.pre_barrier = None

    def op(self, eng, fn, reads=(), writes=(), dma=False):
        o = Op(eng, fn, dma)
        o.idx = self.nops[eng]
        self.nops[eng] += 1
        o.epoch = self.epoch
        deps = []
        for b in reads:
            if b.w is not None:
                deps.append(b.w)
        for b in writes:
            if b.w is not None:
                deps.append(b.w)
            deps.extend(b.r)
        for b in reads:
            b.r.append(o)
        for b in writes:
            b.w = o
            b.r = []
        if dma:
            q = eng
            i = self.drr[q]
            self.drr[q] = (i + 1) % NDMASEM
            o.dsem = self.dsems[q][i]
            prev = self.dlast[q][i]
            if prev is not None:
                deps.append(prev)
            self.dtot[q][i] += 16
            o.dval = self.dtot[q][i]
            self.dlast[q][i] = o
        else:
            self.lastc[eng] = o
        o.deps = deps
        self.ops[eng].append(o)
        return o

    def barrier(self):
        if self.pre_barrier is not None:
            self.pre_barrier()
        lasts = [self.lastc[e] for e in ENGS if self.lastc[e] is not None]
        dmas = [o for q in self.dlast for o in self.dlast[q] if o is not None]
        for e in ENGS:
            o = Op(e, lambda eng: eng.nop(), False)
            o.idx = self.nops[e]
            self.nops[e] += 1
            o.epoch = self.epoch
            o.deps = [x for x in lasts if x.eng != e] + dmas
            self.ops[e].append(o)
        self.epoch += 1

    def flush(self):
        nc = self.nc
        for e in ENGS:
            for o in self.ops[e]:
                need = {}
                dm = []
                for d in o.deps:
                    if d.dma:
                        dm.append(d)
                        continue
                    if d.epoch != o.epoch:
                        continue
                    if d.eng == o.eng and (o.eng == "pe" or (o.idx - d.idx) > 2):
                        continue
                    if d.eng not in need or need[d.eng].idx < d.idx:
                        need[d.eng] = d
                o.deps = list(need.values()) + dm
                for d in need.values():
                    d.sig = True
        for e in ENGS:
            for o in self.ops[e]:
                if o.sig and not o.dma and o.semval is None:
                    self.semcnt[e] += 1
                    o.semval = self.semcnt[e]
        for e in ENGS:
            wd = self.waited[e]
            for o in self.ops[e]:
                need = {}
                for d in o.deps:
                    if d.dma:
                        s, v = d.dsem, d.dval
                    else:
                        assert d.semval is not None
                        s, v = self.sem[d.eng], d.semval
                    k = id(s)
                    if k not in need or need[k][1] < v:
                        need[k] = (s, v)
                for k, (s, v) in need.items():
                    if wd.get(k, 0) >= v:
                        continue
                    wd[k] = v
                    o.waits.append((s, v))
        engmap = {"pe": "tensor", "act": "scalar", "dve": "vector", "pool": "gpsimd", "sp": "sync"}
        with nc.Block() as block:
            for e in ENGS:
                ops = self.ops[e]
                if not ops:
                    continue

                def body(eng, ops=ops, e=e):
                    for o in ops:
                        for (s, v) in o.waits:
                            eng.wait_ge(s, v)
                        ins = o.fn(eng)
                        if o.dma:
                            ins.then_inc(o.dsem, 16)
                        elif o.sig:
                            ins.then_inc(self.sem[e], 1)

                getattr(block, engmap[e])(body)
        for e in ENGS:
            self.ops[e] = []


class Ctx:
    pass


def declare_io(nc, C):
    def din(name, shape, dt=F32):
        return nc.dram_tensor(name, list(shape), dt, kind="ExternalInput").ap()

    C.x = din("x", (S, D))
    C.ln_ffn1 = din("ln_ffn1", (2, D))
    C.w1_gate = din("w1_gate", (2, D, DFF))
    C.w1_up = din("w1_up", (2, D, DFF))
    C.w1_down = din("w1_down", (2, DFF, D))
    C.ln_mix = din("ln_mix", (2, D))
    C.a_w_qkv = din("a_w_qkv", (1, D, 3 * D))
    C.a_w_o = din("a_w_o", (1, D, D))
    C.a_lambda = din("a_lambda", (1, 4, 64))
    C.a_subln = din("a_subln", (1, 128))
    C.b_w_in = din("b_w_in", (1, D, 7 * D))
    C.b_w_o = din("b_w_o", (1, D, D))
    C.ln_ffn2 = din("ln_ffn2", (2, D))
    C.w2_gate = din("w2_gate", (2, D, DFF))
    C.w2_up = din("w2_up", (2, D, DFF))
    C.w2_down = din("w2_down", (2, DFF, D))
    C.ln_final = din("ln_final", (1, D))
    C.ident = din("c_ident", (128, 128), BF16)
    C.cs = din("c_cs", (S, 128))
    C.mask = din("c_mask", (128, 512), BF16)
    C.out = nc.dram_tensor("out", [S, D], F32, kind="ExternalOutput").ap()
    C.xs = nc.dram_tensor("xs", [S, D], F32).ap()


class Alloc:
    _n = [0]

    def __init__(self, nc, es):
        self.nc, self.es = nc, es
        Alloc._n[0] += 1
        self.pfx = "p%d_" % Alloc._n[0]

    def sb(self, name, shape, dt):
        return self.es.enter_context(self.nc.sbuf_tensor(self.pfx + name, list(shape), dt))

    def ps(self, name, shape, dt):
        return self.es.enter_context(self.nc.psum_tensor(self.pfx + name, list(shape), dt))


class NormT:
    def __init__(self, P, nc, C, A, gain_row, x_src, XSRC, transpose=True):
        self.P, self.C, self.x_src, self.XSRC = P, C, x_src, XSRC
        self.xt = [A.sb("xt%d" % i, (128, 4, D), F32) for i in range(2)]
        self.XT = [Buf("xt%d" % i) for i in range(2)]
        self.junk = A.sb("junk", (128, D), BF16)
        self.ss = A.sb("ss", (128, 8), F32)
        self.vv = A.sb("vv", (128, 8), F32)
        self.rstd = A.sb("rstd", (128, 8), F32)
        self.SS = [Buf("ss%d" % i) for i in range(2)]
        self.VV = [Buf("vv%d" % i) for i in range(2)]
        self.RS = [Buf("rs%d" % i) for i in range(2)]
        self.mhalf = A.sb("mhalf", (128, 4), F32)
        self.MH = Buf("mhalf")
        self.gbc = A.sb("gbc", (128, D), F32)
        self.GB = Buf("gbc")
        self.transpose = transpose
        if transpose:
            self.ident = A.sb("ident", (128, 128), BF16)
            self.ID = Buf("ident")
            self.xn2 = [A.sb("xn%d" % i, (128, 4, D), BF16) for i in range(2)]
            self.XN2 = [[Buf("xn%d_%d" % (i, j)) for j in range(4)] for i in range(2)]
            self.xnT = [A.sb("xnT%d" % i, (128, 8, T), BF16) for i in range(2)]
            self.XNT = [Buf("xnT%d" % i) for i in range(2)]
            self.pt = [A.ps("pt%d" % i, (128, D), BF16) for i in range(2)]
            self.PT = [Buf("pt%d" % i) for i in range(2)]
            P.op("sp", lambda e: e.dma_start(out=self.ident[:], in_=C.ident), writes=(self.ID,), dma=True)
        P.op("sp", lambda e: e.dma_start(out=self.gbc[:], in_=gain_row.partition_broadcast(128)), writes=(self.GB,), dma=True)
        P.op("pool", lambda e: e.memset(self.mhalf[:], -0.5), writes=(self.MH,))
        self.cnt = 0

    def load(self, i, b):
        P = self.P
        xt, XT = self.xt, self.XT
        src = self.x_src[i * T:(i + 1) * T, :].rearrange("(j p) d -> p j d", p=128)
        P.op("sp", lambda e: e.dma_start(out=xt[b][:], in_=src), reads=(self.XSRC[i],), writes=(XT[b],), dma=True)

    def stats(self, b):
        P = self.P
        xt, XT, ss, vv, rstd = self.xt, self.XT, self.ss, self.vv, self.rstd
        for j in range(4):
            P.op("act", lambda e, j=j: e.activation(out=self.junk[:], in_=xt[b][:, j, :], func=AF.Square,
                                                    accum_out=ss[:, 4 * b + j:4 * b + j + 1]),
                 reads=(XT[b],), writes=(self.SS[b],))
        P.op("dve", lambda e: e.tensor_scalar(out=vv[:, 4 * b:4 * b + 4], in0=ss[:, 4 * b:4 * b + 4],
                                              scalar1=1.0 / D, scalar2=EPS, op0=ALU.mult, op1=ALU.add),
             reads=(self.SS[b],), writes=(self.VV[b],))
        P.op("pool", lambda e: e.tensor_tensor(out=rstd[:, 4 * b:4 * b + 4], in0=vv[:, 4 * b:4 * b + 4],
                                               in1=self.mhalf[:], op=ALU.pow),
             reads=(self.VV[b], self.MH), writes=(self.RS[b],))

    def prep(self, i, b):
        P = self.P
        self.load(i, b)
        self.stats(b)
        xt, XT, rstd = self.xt, self.XT, self.rstd
        for j in range(4):
            P.op("dve", lambda e, j=j: e.scalar_tensor_tensor(out=self.xn2[b][:, j, :], in0=xt[b][:, j, :],
                                                              scalar=rstd[:, 4 * b + j:4 * b + j + 1], in1=self.gbc[:],
                                                              op0=ALU.mult, op1=ALU.mult),
                 reads=(XT[b], self.RS[b], self.GB), writes=(self.XN2[b][j],))

    def trans(self, b):
        P = self.P
        for j in range(4):
            q = self.cnt % 2
            self.cnt += 1
            for k in range(8):
                P.op("pe", lambda e, j=j, k=k, q=q: e.transpose(out=self.pt[q][:, k * 128:(k + 1) * 128],
                                                                in_=self.xn2[b][:, j, k * 128:(k + 1) * 128],
                                                                identity=self.ident[:]),
                     reads=(self.XN2[b][j], self.ID), writes=(self.PT[q],))
            src = self.pt[q][:].rearrange("p (k t) -> p k t", k=8)
            dst = self.xnT[b][:, :, j * 128:(j + 1) * 128]
            if self.cnt % 2 == 0:
                P.op("act", lambda e, src=src, dst=dst: e.copy(out=dst, in_=src), reads=(self.PT[q],), writes=(self.XNT[b],))
            else:
                P.op("dve", lambda e, src=src, dst=dst: e.tensor_copy(out=dst, in_=src), reads=(self.PT[q],), writes=(self.XNT[b],))

    def load_norm(self, i, b):
        self.prep(i, b)
        self.trans(b)


def ffn_phase(P, nc, C, x_src, XSRC, x_dst, XDST, gain_row, wg, wu, wd, final_gain=None, bg=None):
    with ExitStack() as es:
        A = Alloc(nc, es)
        N = NormT(P, nc, C, A, gain_row, x_src, XSRC)
        xt, XT, xnT, XNT = N.xt, N.XT, N.xnT, N.XNT
        NW = 3
        wgt = [A.sb("wgt%d" % i, (128, 8, 256), BF16) for i in range(NW)]
        wut = [A.sb("wut%d" % i, (128, 8, 256), BF16) for i in range(NW)]
        WG = [Buf("wg%d" % i) for i in range(NW)]
        WU = [Buf("wu%d" % i) for i in range(NW)]
        wdt = A.sb("wdt", (128, NFC, D), BF16)
        WD = Buf("wdt")
        hT = A.sb("hT", (128, NFC, T), BF16)
        HT = [Buf("hT%d" % c) for c in range(NFC)]
        sg = [A.sb("sg%d" % i, (128, T), F32) for i in range(2)]
        SG = [Buf("sg%d" % i) for i in range(2)]
        pg = [A.ps("pg%d" % i, (128, T), F32) for i in range(2)]
        PG = [Buf("pg%d" % i) for i in range(2)]
        pu = [A.ps("pu%d" % i, (128, T), F32) for i in range(2)]
        PU = [Buf("pu%d" % i) for i in range(2)]
        pd = [A.ps("pd%d" % i, (128, T), F32) for i in range(2)]
        PD = [Buf("pd%d" % i) for i in range(2)]
        if final_gain is not None:
            fgb = A.sb("fgb", (128, D), F32)
            FG = Buf("fgb")
            fss = A.sb("fss", (128, 8), F32)
            fvv = A.sb("fvv", (128, 8), F32)
            frs = A.sb("frs", (128, 8), F32)
            FSS = [Buf("fss%d" % i) for i in range(2)]
            FVV = [Buf("fvv%d" % i) for i in range(2)]
            FRS = [Buf("frs%d" % i) for i in range(2)]
            P.op("sp", lambda e: e.dma_start(out=fgb[:], in_=final_gain.partition_broadcast(128)), writes=(FG,), dma=True)

        wd_v = wd.rearrange("(c p) d -> p c d", p=128)
        half = NFC // 2
        P.op("sp", lambda e: e.dma_start(out=wdt[:, 0:half, :], in_=wd_v[:, 0:half, :]), writes=(WD,), dma=True)
        P.op("sp", lambda e: e.dma_start(out=wdt[:, half:NFC, :], in_=wd_v[:, half:NFC, :]), writes=(WD,), dma=True)
        wg_v = wg.rearrange("(k p) f -> p k f", p=128)
        wu_v = wu.rearrange("(k p) f -> p k f", p=128)
        cnt = {"w": 0, "pp": 0, "pd": 0}

        def gate_up(i, b):
            for grp in range(NFC // 2):
                if grp == 3 and i + 1 < NT:
                    N.prep(i + 1, 1 - b)
                s = cnt["w"] % NW
                cnt["w"] += 1
                c0 = grp * 256
                P.op("sp", lambda e, s=s, c0=c0: e.dma_start(out=wgt[s][:], in_=wg_v[:, :, c0:c0 + 256]),
                     writes=(WG[s],), dma=True)
                P.op("sp", lambda e, s=s, c0=c0: e.dma_start(out=wut[s][:], in_=wu_v[:, :, c0:c0 + 256]),
                     writes=(WU[s],), dma=True)
                for c in range(2):
                    fc = grp * 2 + c
                    q = cnt["pp"] % 2
                    cnt["pp"] += 1
                    for k in range(8):
                        P.op("pe", lambda e, s=s, c=c, k=k, q=q: e.matmul(pg[q][:], lhsT=wgt[s][:, k, c * 128:(c + 1) * 128],
                                                                          rhs=xnT[b][:, k, :], start=(k == 0), stop=(k == 7)),
                             reads=(WG[s], XNT[b]), writes=(PG[q],))
                    for k in range(8):
                        P.op("pe", lambda e, s=s, c=c, k=k, q=q: e.matmul(pu[q][:], lhsT=wut[s][:, k, c * 128:(c + 1) * 128],
                                                                          rhs=xnT[b][:, k, :], start=(k == 0), stop=(k == 7)),
                             reads=(WU[s], XNT[b]), writes=(PU[q],))
                    P.op("act", lambda e, q=q: e.activation(out=sg[q][:], in_=pg[q][:], func=AF.Silu),
                         reads=(PG[q],), writes=(SG[q],))
                    P.op("dve", lambda e, q=q, fc=fc: e.tensor_tensor(out=hT[:, fc, :], in0=sg[q][:], in1=pu[q][:], op=ALU.mult),
                         reads=(SG[q], PU[q]), writes=(HT[fc],))

        def down_res(i, b):
            for j in range(4):
                for h in range(2):
                    q = cnt["pd"] % 2
                    cnt["pd"] += 1
                    for fc in range(NFC):
                        P.op("pe", lambda e, j=j, h=h, fc=fc, q=q: e.matmul(pd[q][:], lhsT=hT[:, fc, j * 128:(j + 1) * 128],
                                                                            rhs=wdt[:, fc, h * 512:(h + 1) * 512],
                                                                            start=(fc == 0), stop=(fc == NFC - 1)),
                             reads=(HT[fc], WD), writes=(PD[q],))
                    P.op("dve", lambda e, j=j, h=h, q=q: e.scalar_tensor_tensor(
                        out=xt[b][:, j, h * 512:(h + 1) * 512], in0=pd[q][:], scalar=0.5,
                        in1=xt[b][:, j, h * 512:(h + 1) * 512], op0=ALU.mult, op1=ALU.add),
                         reads=(PD[q], XT[b]), writes=(XT[b],))
            if final_gain is not None:
                for j in range(4):
                    P.op("act", lambda e, j=j: e.activation(out=N.junk[:], in_=xt[b][:, j, :], func=AF.Square,
                                                            accum_out=fss[:, 4 * b + j:4 * b + j + 1]),
                         reads=(XT[b],), writes=(FSS[b],))
                P.op("dve", lambda e: e.tensor_scalar(out=fvv[:, 4 * b:4 * b + 4], in0=fss[:, 4 * b:4 * b + 4],
                                                      scalar1=1.0 / D, scalar2=EPS, op0=ALU.mult, op1=ALU.add),
                     reads=(FSS[b],), writes=(FVV[b],))
                P.op("pool", lambda e: e.tensor_tensor(out=frs[:, 4 * b:4 * b + 4], in0=fvv[:, 4 * b:4 * b + 4],
                                                       in1=N.mhalf[:], op=ALU.pow),
                     reads=(FVV[b], N.MH), writes=(FRS[b],))
                for j in range(4):
                    P.op("dve", lambda e, j=j: e.scalar_tensor_tensor(out=xt[b][:, j, :], in0=xt[b][:, j, :],
                                                                      scalar=frs[:, 4 * b + j:4 * b + j + 1], in1=fgb[:],
                                                                      op0=ALU.mult, op1=ALU.mult),
                         reads=(XT[b], FRS[b], FG), writes=(XT[b],))
            dst = x_dst[i * T:(i + 1) * T, :].rearrange("(j p) d -> p j d", p=128)
            P.op("pool", lambda e: e.dma_start(out=dst, in_=xt[b][:]), reads=(XT[b],), writes=(XDST[i],), dma=True)
            if bg is not None:
                bg.tick(2)

        N.load_norm(0, 0)
        for i in range(NT):
            b = i % 2
            gate_up(i, b)
            if i + 1 < NT:
                N.trans(1 - b)
            down_res(i, b)
        P.barrier()
        P.flush()


def qkv_phase(P, nc, C, x_src, XSRC, gain_row, roped, vspec, bg=None):
    with ExitStack() as es:
        A = Alloc(nc, es)
        N = NormT(P, nc, C, A, gain_row, x_src, XSRC)
        xnT, XNT = N.xnT, N.XNT
        NWB = 3
        wt = [A.sb("wt%d" % i, (128, 8, D), BF16) for i in range(NWB)]
        WT = [Buf("wt%d" % i) for i in range(NWB)]
        cs = [A.sb("cs%d" % i, (128, 4, 128), F32) for i in range(3)]
        CS = [Buf("cs%d" % i) for i in range(3)]
        xr = [A.sb("xr%d" % i, (128, D), F32) for i in range(3)]
        XR = [Buf("xr%d" % i) for i in range(3)]
        t1 = [A.sb("t1_%d" % i, (128, D), F32) for i in range(2)]
        T1 = [Buf("t1_%d" % i) for i in range(2)]
        t2 = [A.sb("t2_%d" % i, (128, D), F32) for i in range(2)]
        T2 = [Buf("t2_%d" % i) for i in range(2)]
        ro = [A.sb("ro%d" % i, (128, D), BF16) for i in range(2)]
        RO = [Buf("ro%d" % i) for i in range(2)]
        stg = [A.sb("stg%d" % i, (128, 8, T), BF16) for i in range(2)]
        STG = [Buf("stg%d" % i) for i in range(2)]
        vst = [A.sb("vst%d" % i, (128, 4, D), BF16) for i in range(2)]
        VST = [Buf("vst%d" % i) for i in range(2)]
        pr = [A.ps("pr%d" % i, (128, D), F32) for i in range(2)]
        PR = [Buf("pr%d" % i) for i in range(2)]
        pq = [A.ps("pq%d" % i, (128, D), BF16) for i in range(2)]
        PQ = [Buf("pq%d" % i) for i in range(2)]
        (vw, vdst, VDST, vmode) = vspec
        mats = [(w, dst, DST, "r") for (w, dst, DST) in roped] + [(vw, vdst, VDST, "v")]
        NM = len(mats)
        items = []
        for i in range(NT):
            for mi, (w, dst, DST, kind) in enumerate(mats):
                for j in range(4):
                    items.append({"i": i, "mi": mi, "j": j, "kind": kind, "g": i * NM + mi})
        NI = len(items)

        def load_w(gidx):
            i, mi = divmod(gidx, NM)
            if i >= NT:
                return
            s = gidx % NWB
            wv = mats[mi][0].rearrange("(k p) f -> p k f", p=128)
            P.op("sp", lambda e: e.dma_start(out=wt[s][:, 0:4, :], in_=wv[:, 0:4, :]), writes=(WT[s],), dma=True)
            P.op("sp", lambda e: e.dma_start(out=wt[s][:, 4:8, :], in_=wv[:, 4:8, :]), writes=(WT[s],), dma=True)

        def load_cs(i):
            if i >= NT:
                return
            cb = i % 3
            csrc = C.cs[i * T:(i + 1) * T, :].rearrange("(j p) c -> p j c", p=128)
            P.op("sp", lambda e: e.dma_start(out=cs[cb][:], in_=csrc), writes=(CS[cb],), dma=True)

        def st_a(n):
            it = items[n]
            i, j, s, b = it["i"], it["j"], it["g"] % NWB, it["i"] % 2
            if it["j"] == 0:
                if it["mi"] == 0:
                    if i + 1 < NT:
                        N.prep(i + 1, 1 - b)
                    load_cs(i + 1)
                    if bg is not None:
                        bg.tick(2)
                if it["mi"] == NM - 1 and i + 1 < NT:
                    N.trans(1 - b)
                load_w(it["g"] + 1)
            q = n % 2
            for h in range(2):
                for k in range(8):
                    P.op("pe", lambda e, h=h, k=k: e.matmul(pr[q][:, h * 512:(h + 1) * 512], lhsT=xnT[b][:, k, j * 128:(j + 1) * 128],
                                                            rhs=wt[s][:, k, h * 512:(h + 1) * 512], start=(k == 0), stop=(k == 7)),
                         reads=(XNT[b], WT[s]), writes=(PR[q],))

        def st_b(n):
            it = items[n]
            q = n % 2
            if it["kind"] == "v":
                vs_ = it["i"] % 2
                j, i = it["j"], it["i"]
                P.op("act", lambda e: e.copy(out=vst[vs_][:, j, :], in_=pr[q][:]), reads=(PR[q],), writes=(VST[vs_],))
                if j == 3:
                    t0 = i * T
                    if vmode == "nat":
                        dv = vdst[t0:t0 + T, :].rearrange("(j p) d -> p j d", p=128)
                        P.op("pool", lambda e: e.dma_start(out=dv, in_=vst[vs_][:]), reads=(VST[vs_],), writes=(VDST,), dma=True)
                    else:
                        for jj in range(4):
                            dv = vdst.rearrange("h p c e -> p c h e")[:, 4 * i + jj, :, :]
                            P.op("pool", lambda e, dv=dv, jj=jj: e.dma_start(
                                out=dv, in_=vst[vs_][:, jj, :].rearrange("p (h e) -> p h e", h=8)), reads=(VST[vs_],), writes=(VDST,), dma=True)
                return
            u = n % 3
            P.op("act", lambda e: e.copy(out=xr[u][:], in_=pr[q][:]), reads=(PR[q],), writes=(XR[u],))

        def st_c(n):
            it = items[n]
            if it["kind"] == "v":
                return
            u, v, cb, j = n % 3, n % 2, it["i"] % 3, it["j"]
            cbc = cs[cb][:, j, 0:64].unsqueeze(1).to_broadcast([128, 16, 64])
            P.op("dve", lambda e: e.tensor_tensor(out=t1[v][:].rearrange("p (a c) -> p a c", a=16),
                                                  in0=xr[u][:].rearrange("p (a c) -> p a c", a=16), in1=cbc, op=ALU.mult),
                 reads=(XR[u], CS[cb]), writes=(T1[v],))
            xv = xr[u][:].rearrange("p (a z c) -> p a z c", a=16, z=2)
            tv = t2[v][:].rearrange("p (a z c) -> p a z c", a=16, z=2)
            s0 = cs[cb][:, j, 64:96].unsqueeze(1).to_broadcast([128, 16, 32])
            s1 = cs[cb][:, j, 96:128].unsqueeze(1).to_broadcast([128, 16, 32])
            P.op("pool", lambda e: e.tensor_tensor(out=tv[:, :, 0, :], in0=xv[:, :, 1, :], in1=s0, op=ALU.mult),
                 reads=(XR[u], CS[cb]), writes=(T2[v],))
            P.op("pool", lambda e: e.tensor_tensor(out=tv[:, :, 1, :], in0=xv[:, :, 0, :], in1=s1, op=ALU.mult),
                 reads=(XR[u], CS[cb]), writes=(T2[v],))

        def st_d(n):
            it = items[n]
            if it["kind"] == "v":
                return
            v = n % 2
            P.op("dve", lambda e: e.tensor_tensor(out=ro[v][:], in0=t1[v][:], in1=t2[v][:], op=ALU.add),
                 reads=(T1[v], T2[v]), writes=(RO[v],))

        def st_e(n):
            it = items[n]
            if it["kind"] == "v":
                return
            v = n % 2
            for k in range(8):
                P.op("pe", lambda e, k=k: e.transpose(out=pq[v][:, k * 128:(k + 1) * 128], in_=ro[v][:, k * 128:(k + 1) * 128],
                                                      identity=N.ident[:]), reads=(RO[v], N.ID), writes=(PQ[v],))

        def st_f(n):
            it = items[n]
            if it["kind"] == "v":
                return
            v, j, i = n % 2, it["j"], it["i"]
            sg_ = it["g"] % 2
            P.op("act", lambda e: e.copy(out=stg[sg_][:, :, j * 128:(j + 1) * 128], in_=pq[v][:].rearrange("p (k t) -> p k t", k=8)),
                 reads=(PQ[v],), writes=(STG[sg_],))
            if j == 3:
                (w, dst, DST, kind) = mats[it["mi"]]
                dv = dst.rearrange("h p t -> p h t")[:, :, i * T:(i + 1) * T]
                P.op("pool", lambda e: e.dma_start(out=dv, in_=stg[sg_][:]), reads=(STG[sg_],), writes=(DST,), dma=True)

        N.load_norm(0, 0)
        load_cs(0)
        load_w(0)
        stages = [st_a, st_b, st_c, st_d, st_e, st_f]
        for it_ in range(NI + len(stages) - 1):
            for si, st in enumerate(stages):
                n = it_ - si
                if 0 <= n < NI:
                    st(n)
        P.barrier()
        P.flush()


def att0_phase(P, nc, C, qT, kT, Vs, attnT, ATT, lam_src, lambda_init, bg=None):
    NQ = S // 512
    NK = S // 128
    with ExitStack() as es:
        A = Alloc(nc, es)
        qh = [A.sb("qh%d" % i, (128, S), BF16) for i in range(2)]
        QH = [Buf("qh%d" % i) for i in range(2)]
        kp = [[A.sb("kp%d_%d" % (i, c), (128, S), BF16) for c in range(1)] for i in range(2)]
        KP = [[Buf("kp%d_%d" % (i, c)) for c in range(1)] for i in range(2)]
        vh = [A.sb("vh%d" % i, (128, NK, 128), BF16) for i in range(2)]
        VH = [Buf("vh%d" % i) for i in range(2)]
        NPB = 4
        ptb = [A.sb("ptb%d" % i, (128, 1024), BF16) for i in range(NPB)]
        PTB = [Buf("ptb%d" % i) for i in range(NPB)]
        ones = A.sb("onesf", (128, 128), F32)
        ON = Buf("onesf")
        onesb = A.sb("onesb", (128, 128), BF16)
        ONB_ = Buf("onesb")
        ad = [A.sb("ad%d" % i, (128, 512), F32) for i in range(2)]
        AD = [Buf("ad%d" % i) for i in range(2)]
        oc = A.sb("oc", (128, 1024), F32)
        OC = Buf("oc")
        dc = A.sb("dc", (128, 1024), F32)
        DC = Buf("dc")
        rc = A.sb("rc", (128, 1024), F32)
        RC = Buf("rc")
        ta = A.sb("ta", (128, 512), F32)
        TA = Buf("ta")
        tb = A.sb("tb", (128, 512), F32)
        TB = Buf("tb")
        ob = [A.sb("ob%d" % i, (128, 512), BF16) for i in range(2)]
        OB = [Buf("ob%d" % i) for i in range(2)]
        lamb = A.sb("lamb", (128, 256), F32)
        lprod = A.sb("lprod", (128, 128), F32)
        lsum = A.sb("lsum", (128, 2), F32)
        lex = A.sb("lex", (128, 2), F32)
        nl = A.sb("nl", (128, 1), F32)
        LB, LP, LS, LE, NL = Buf("lamb"), Buf("lprod"), Buf("lsum"), Buf("lex"), Buf("nl")
        sp_ = [A.ps("sp%d" % i, (128, 1024), F32) for i in range(2)]
        SP_ = [Buf("sp%d" % i) for i in range(2)]
        po = A.ps("po", (128, 1024), F32)
        PO = Buf("po")
        pn = A.ps("pn", (128, 1024), F32)
        PN = Buf("pn")

        P.op("dve", lambda e: e.memset(ones[:], 1.0), writes=(ON,))
        P.op("dve", lambda e: e.memset(onesb[:], 1.0), writes=(ONB_,))
        PN1 = Buf("pn1")
        lsrc = lam_src.rearrange("a b -> (a b)").unsqueeze(0)
        P.op("sp", lambda e: e.dma_start(out=lamb[:], in_=lsrc.partition_broadcast(128)), writes=(LB,), dma=True)
        lv = lamb[:].rearrange("p (a b c) -> p a b c", a=2, b=2)
        P.op("dve", lambda e: e.tensor_tensor(out=lprod[:].rearrange("p (a c) -> p a c", a=2), in0=lv[:, :, 0, :], in1=lv[:, :, 1, :], op=ALU.mult),
             reads=(LB,), writes=(LP,))
        P.op("dve", lambda e: e.reduce_sum(out=lsum[:], in_=lprod[:].rearrange("p (a c) -> p a c", a=2), axis=mybir.AxisListType.X),
             reads=(LP,), writes=(LS,))
        P.op("act", lambda e: e.activation(out=lex[:], in_=lsum[:], func=AF.Exp), reads=(LS,), writes=(LE,))
        P.op("dve", lambda e: e.scalar_tensor_tensor(out=nl[:], in0=lex[:, 1:2], scalar=-float(lambda_init), in1=lex[:, 0:1],
                                                     op0=ALU.add, op1=ALU.subtract), reads=(LE,), writes=(NL,))
        cnt = {"s": 0, "p": 0, "ob": 0}

        def load_head(h, hb):
            P.op("sp", lambda e: e.dma_start(out=qh[hb][:], in_=qT[h]), writes=(QH[hb],), dma=True)
            P.op("sp", lambda e: e.dma_start(out=kp[hb][0][:], in_=kT[h]), writes=(KP[hb][0],), dma=True)
            P.op("sp", lambda e: e.dma_start(out=vh[hb][:], in_=Vs[h]), writes=(VH[hb],), dma=True)

        def s_mm(hb, qt, kc):
            sb_ = cnt["s"] % 2
            cnt["s"] += 1
            for c in range(2):
                P.op("pe", lambda e, c=c, sb_=sb_: e.matmul(sp_[sb_][:, c * 512:(c + 1) * 512],
                                                            lhsT=kp[hb][0][c * 64:(c + 1) * 64, kc * 128:(kc + 1) * 128],
                                                            rhs=qh[hb][c * 64:(c + 1) * 64, qt * 512:(qt + 1) * 512], start=True, stop=True),
                     reads=(KP[hb][0], QH[hb]), writes=(SP_[sb_],))
            return sb_


        def exp_pv(hb, kc, sb_, x):
            pb = cnt["p"] % NPB
            cnt["p"] += 1
            P.op("act", lambda e: e.activation(out=ptb[pb][:], in_=sp_[sb_][:], func=AF.Exp, scale=0.125),
                 reads=(SP_[sb_],), writes=(PTB[pb],))
            for c in range(2):
                P.op("pe", lambda e, c=c: e.matmul(po[:, c * 512:(c + 1) * 512], lhsT=vh[hb][:, kc, :],
                                                   rhs=ptb[pb][:, c * 512:(c + 1) * 512], start=(kc == 0), stop=(kc == NK - 1)),
                     reads=(VH[hb], PTB[pb]), writes=(PO,))
            P.op("pe", lambda e: e.matmul(pn[:, 512:1024], lhsT=onesb[:], rhs=ptb[pb][:, 512:1024], start=(kc == 0), stop=(kc == NK - 1)),
                 reads=(ONB_, PTB[pb]), writes=(PN1,))
            if kc == 0:
                P.op("dve", lambda e: e.tensor_copy(out=ad[x][:], in_=ptb[pb][:, 0:512]), reads=(PTB[pb],), writes=(AD[x],))
            else:
                P.op("dve", lambda e: e.tensor_tensor(out=ad[x][:], in0=ad[x][:], in1=ptb[pb][:, 0:512], op=ALU.add),
                     reads=(PTB[pb], AD[x]), writes=(AD[x],))

        def epi1(x):
            P.op("pe", lambda e: e.matmul(pn[:, 0:512], lhsT=ones[:], rhs=ad[x][:], start=True, stop=True),
                 reads=(ON, AD[x]), writes=(PN,))
            P.op("dve", lambda e: e.tensor_copy(out=dc[:, 512:1024], in_=pn[:, 512:1024]), reads=(PN1,), writes=(DC,))
            P.op("dve", lambda e: e.tensor_copy(out=oc[:], in_=po[:]), reads=(PO,), writes=(OC,))

        def epi2(h, qt):
            P.op("act", lambda e: e.activation(out=rc[:, 0:512], in_=pn[:, 0:512], func=AF.Ln), reads=(PN,), writes=(RC,))
            P.op("act", lambda e: e.activation(out=rc[:, 512:1024], in_=dc[:, 512:1024], func=AF.Ln), reads=(DC,), writes=(RC,))
            P.op("act", lambda e: e.activation(out=rc[:], in_=rc[:], func=AF.Exp, scale=-1.0), reads=(RC,), writes=(RC,))
            P.op("dve", lambda e: e.tensor_tensor(out=ta[:], in0=oc[:, 0:512], in1=rc[:, 0:512], op=ALU.mult), reads=(OC, RC), writes=(TA,))
            P.op("dve", lambda e: e.tensor_tensor(out=tb[:], in0=oc[:, 512:1024], in1=rc[:, 512:1024], op=ALU.mult), reads=(OC, RC), writes=(TB,))
            o_ = cnt["ob"] % 2
            cnt["ob"] += 1
            P.op("dve", lambda e: e.scalar_tensor_tensor(out=ob[o_][:], in0=tb[:], scalar=nl[:, 0:1], in1=ta[:], op0=ALU.mult, op1=ALU.add),
                 reads=(TA, TB, NL), writes=(OB[o_],))
            P.op("pool", lambda e: e.dma_start(out=attnT[h][:, qt * 512:(qt + 1) * 512], in_=ob[o_][:]), reads=(OB[o_],), writes=(ATT,), dma=True)
            if bg is not None:
                bg.tick(1)

        load_head(0, 0)
        deferred = []
        nu = 0
        for h in range(8):
            hb = h % 2
            if h + 1 < 8:
                load_head(h + 1, 1 - hb)
            for qt in range(NQ):
                x = nu % 2
                nu += 1
                sb_ = s_mm(hb, qt, 0)
                for kc in range(NK):
                    nxt = s_mm(hb, qt, kc + 1) if kc + 1 < NK else None
                    exp_pv(hb, kc, sb_, x)
                    sb_ = nxt
                    if kc == 3 and deferred:
                        deferred.pop(0)()
                epi1(x)
                deferred.append(lambda h=h, qt=qt: epi2(h, qt))
        while deferred:
            deferred.pop(0)()
        P.barrier()
        P.flush()


def oproj_phase(P, nc, C, x, XB, attnT, wo, subln=None, bg=None):
    with ExitStack() as es:
        A = Alloc(nc, es)
        xt = [A.sb("xt%d" % i, (128, 4, D), F32) for i in range(2)]
        XT = [Buf("xt%d" % i) for i in range(2)]
        at = [A.sb("at%d" % i, (128, 8, T), BF16) for i in range(2)]
        AT = [Buf("at%d" % i) for i in range(2)]
        wot = A.sb("wot", (128, 8, D), BF16)
        WO = Buf("wot")
        po_ = [A.ps("po%d" % i, (128, 512), F32) for i in range(2)]
        PO_ = [Buf("po%d" % i) for i in range(2)]
        wv = wo.rearrange("(k p) f -> p k f", p=128)
        P.op("sp", lambda e: e.dma_start(out=wot[:, 0:4, :], in_=wv[:, 0:4, :]), writes=(WO,), dma=True)
        P.op("sp", lambda e: e.dma_start(out=wot[:, 4:8, :], in_=wv[:, 4:8, :]), writes=(WO,), dma=True)
        if subln is not None:
            (sub_row, scale) = subln
            sq = [A.sb("sq%d" % i, (128, 512), BF16) for i in range(2)]
            SQ = [Buf("sq%d" % i) for i in range(2)]
            v_ = [A.sb("v%d" % i, (128, 512), F32) for i in range(2)]
            V_ = [Buf("v%d" % i) for i in range(2)]
            r_ = [A.sb("r%d" % i, (128, 512), F32) for i in range(2)]
            R_ = [Buf("r%d" % i) for i in range(2)]
            on = [A.sb("on%d" % i, (128, 8, T), BF16) for i in range(2)]
            ONB = [[Buf("on%d_%d" % (i, h)) for h in range(8)] for i in range(2)]
            gcol = A.sb("gcol", (128, 1), F32)
            gcs = A.sb("gcs", (128, 1), F32)
            GC, GCS = Buf("gcol"), Buf("gcs")
            ones = A.sb("ones", (128, 128), BF16)
            ONE = Buf("ones")
            pm = [A.ps("pm%d" % i, (128, 512), F32) for i in range(2)]
            PM = [Buf("pm%d" % i) for i in range(2)]
            P.op("sp", lambda e: e.dma_start(out=gcol[:], in_=sub_row.rearrange("a e -> e a")), writes=(GC,), dma=True)
            P.op("dve", lambda e: e.tensor_scalar(out=gcs[:], in0=gcol[:], scalar1=float(scale), scalar2=None, op0=ALU.mult), reads=(GC,), writes=(GCS,))
            epst = A.sb("epst", (128, 1), F32)
            EPT = Buf("epst")
            P.op("pool", lambda e: e.memset(epst[:], EPS), writes=(EPT,))
            P.op("dve", lambda e: e.memset(ones[:], 1.0), writes=(ONE,))
        cnt = {"q": 0, "m": 0}

        def load(i, b):
            t0 = i * T
            P.op("sp", lambda e: e.dma_start(out=xt[b][:], in_=x[t0:t0 + T, :].rearrange("(j p) d -> p j d", p=128)),
                 reads=(XB[i],), writes=(XT[b],), dma=True)
            P.op("sp", lambda e: e.dma_start(out=at[b][:], in_=attnT.rearrange("h p t -> p h t")[:, :, t0:t0 + T]),
                 writes=(AT[b],), dma=True)

        def proj(i, b, src, SRC):
            t0 = i * T
            for j in range(4):
                for hf in range(2):
                    q = cnt["q"] % 2
                    cnt["q"] += 1
                    for h in range(8):
                        P.op("pe", lambda e, j=j, hf=hf, h=h, q=q: e.matmul(po_[q][:], lhsT=src[:, h, j * 128:(j + 1) * 128],
                                                                            rhs=wot[:, h, hf * 512:(hf + 1) * 512],
                                                                            start=(h == 0), stop=(h == 7)),
                             reads=(SRC[h], WO), writes=(PO_[q],))
                    P.op("dve", lambda e, j=j, hf=hf, q=q: e.tensor_tensor(out=xt[b][:, j, hf * 512:(hf + 1) * 512], in0=po_[q][:],
                                                                           in1=xt[b][:, j, hf * 512:(hf + 1) * 512], op=ALU.add),
                         reads=(PO_[q], XT[b]), writes=(XT[b],))
            P.op("pool", lambda e: e.dma_start(out=x[t0:t0 + T, :].rearrange("(j p) d -> p j d", p=128), in_=xt[b][:]),
                 reads=(XT[b],), writes=(XB[i],), dma=True)
            if bg is not None:
                bg.tick(1)

        if subln is None:
            load(0, 0)
            for i in range(NT):
                b = i % 2
                if i + 1 < NT:
                    load(i + 1, 1 - b)
                proj(i, b, at[b], [AT[b]] * 8)
        else:
            items = [(i, h) for i in range(NT) for h in range(8)]
            NI = len(items)

            def s0(n):
                i, h = items[n]
                b, m = i % 2, n % 2
                P.op("dve", lambda e: e.tensor_tensor(out=sq[m][:], in0=at[b][:, h, :], in1=at[b][:, h, :], op=ALU.mult),
                     reads=(AT[b],), writes=(SQ[m],))

            def s1(n):
                m = n % 2
                P.op("pe", lambda e: e.matmul(pm[m][:], lhsT=ones[:], rhs=sq[m][:], start=True, stop=True),
                     reads=(ONE, SQ[m]), writes=(PM[m],))

            def s2(n):
                m = n % 2
                P.op("act", lambda e: e.activation(out=v_[m][:], in_=pm[m][:], func=AF.Ln, scale=1.0 / 128, bias=epst[:, 0:1]),
                     reads=(PM[m], EPT), writes=(V_[m],))
                P.op("act", lambda e: e.activation(out=r_[m][:], in_=v_[m][:], func=AF.Exp, scale=-0.5),
                     reads=(V_[m],), writes=(R_[m],))

            def s3(n):
                i, h = items[n]
                b, m = i % 2, n % 2
                P.op("dve", lambda e: e.scalar_tensor_tensor(out=on[b][:, h, :], in0=at[b][:, h, :], scalar=gcs[:, 0:1],
                                                             in1=r_[m][:], op0=ALU.mult, op1=ALU.mult),
                     reads=(AT[b], GCS, R_[m]), writes=(ONB[b][h],))
                if h == 7:
                    proj(i, b, on[b], ONB[b])
                    if i + 2 < NT:
                        load(i + 2, b)

            load(0, 0)
            load(1, 1)
            stages = [s0, s1, s2, s3]
            for it_ in range(NI + len(stages) - 1):
                for si, st in enumerate(stages):
                    n = it_ - si
                    if 0 <= n < NI:
                        st(n)
        P.barrier()
        P.flush()


def att1_phase(P, nc, C, qk, vn1, attnT, ATT):
    DIL = (1, 4, 16)
    with ExitStack() as es:
        A = Alloc(nc, es)
        acc2 = [A.sb("acc%d" % i, (128, 2, S), F32) for i in range(2)]
        ACC2 = [Buf("acc%d" % i) for i in range(2)]
        qg = [A.sb("qg%d" % i, (128, S), BF16) for i in range(2)]
        QG = [Buf("qg%d" % i) for i in range(2)]
        kp = [[A.sb("kp%d_%d" % (i, c), (128, S), BF16) for c in range(2)] for i in range(2)]
        KP = [[Buf("kp%d_%d" % (i, c)) for c in range(2)] for i in range(2)]
        vg = [A.sb("vg%d" % i, (128, 32, 128), BF16) for i in range(3)]
        VG = [Buf("vg%d" % i) for i in range(3)]
        vp = [[A.sb("vp%d_%d" % (i, c), (128, 32, 128), BF16) for c in range(2)] for i in range(2)]
        VP = [[Buf("vp%d_%d" % (i, c)) for c in range(2)] for i in range(2)]
        NPB = 3
        ptb = [A.sb("ptb%d" % i, (128, 2, 512), BF16) for i in range(NPB)]
        PTB = [[Buf("ptb%d_%d" % (i, c)) for c in range(2)] for i in range(NPB)]
        mask = A.sb("mask", (128, 512), BF16)
        MK = Buf("mask")
        onesp = [A.sb("onesp%d" % c, (128, 128), BF16) for c in range(2)]
        ONP = [Buf("onesp%d" % c) for c in range(2)]
        lnd = A.sb("lnd", (128, S), F32)
        LND = Buf("lnd")
        outb = A.sb("outb", (128, S), BF16)
        OUTB = Buf("outb")
        sps = [A.ps("sps%d" % i, (128, 2, 512), F32) for i in range(2)]
        SPS = [Buf("sps%d" % i) for i in range(2)]
        pz = [A.ps("pz%d" % i, (128, 2, 512), F32) for i in range(2)]
        PZ = [Buf("pz%d" % i) for i in range(2)]

        identb = A.sb("identb", (128, 128), BF16)
        IDB = Buf("identb")
        P.op("sp", lambda e: e.dma_start(out=mask[:], in_=C.mask), writes=(MK,), dma=True)
        P.op("sp", lambda e: e.dma_start(out=identb[:], in_=C.ident), writes=(IDB,), dma=True)
        for c in range(2):
            P.op("pool", lambda e, c=c: e.memset(onesp[c][:], 0.0), writes=(ONP[c],))
            P.op("pool", lambda e, c=c: e.memset(onesp[c][:, c * 64:(c + 1) * 64], 1.0), writes=(ONP[c],))
        for i in range(2):
            P.op("pool", lambda e, i=i: e.memset(kp[i][0][64:128, :], 0.0), writes=(KP[i][0],))
            P.op("pool", lambda e, i=i: e.memset(kp[i][1][0:64, :], 0.0), writes=(KP[i][1],))
            P.op("pool", lambda e, i=i: e.memset(vp[i][0][:, :, 64:128], 1.0), writes=(VP[i][0],))
            P.op("pool", lambda e, i=i: e.memset(vp[i][1][:, :, 0:64], 1.0), writes=(VP[i][1],))
        cnt = {"s": 0, "p": 0, "z": 0}
        seq = [(hp, g) for hp in range(8) for g in range(3)]

        def load(hp, g, b):
            d = DIL[g]
            nj = (S // d) // 128
            q, k = qk[2 * g], qk[2 * g + 1]
            P.op("sp", lambda e: e.dma_start(out=qg[b][:], in_=q[hp]), writes=(QG[b],), dma=True)
            P.op("sp", lambda e: e.dma_start(out=kp[b][0][0:64, :], in_=k[hp][0:64, :]), writes=(KP[b][0],), dma=True)
            P.op("sp", lambda e: e.dma_start(out=kp[b][1][64:128, :], in_=k[hp][64:128, :]), writes=(KP[b][1],), dma=True)

        def load_v(n):
            hp, g = seq[n]
            d = DIL[g]
            nj = (S // d) // 128
            vb = n % 3
            vsrc = vn1.rearrange("(j a p) c -> p a j c", a=128, p=d)
            for p in range(d):
                P.op("sp", lambda e, p=p: e.dma_start(out=vg[vb][:, p * nj:(p + 1) * nj, :],
                                                      in_=vsrc[p, :, :, hp * 128:(hp + 1) * 128]),
                     writes=(VG[vb],), dma=True)

        def build_vp(n):
            b, vb = n % 2, n % 3
            for c in range(2):
                P.op("act", lambda e, c=c: e.copy(out=vp[b][c][:, :, c * 64:(c + 1) * 64], in_=vg[vb][:, :, c * 64:(c + 1) * 64]),
                     reads=(VG[vb],), writes=(VP[b][c],))

        def units_for(g):
            d = DIL[g]
            L = S // d
            nj = L // 128
            bl = []
            for p in range(d):
                us = []
                for jj in range(nj):
                    lo, hi = 128 * jj - 64, 128 * jj + 192
                    off = max(0, -lo)
                    lo, hi = max(lo, 0), min(hi, L)
                    us.append((lo, hi - lo, off, jj))
                for n in range(0, len(us), 2):
                    bl.append((p, us[n:n + 2]))
            return bl

        def s_mm(b, g, batch):
            d = DIL[g]
            p, us = batch
            sb_ = cnt["s"] % 2
            cnt["s"] += 1
            for u, (mq0, nq, off, jj) in enumerate(us):
                qs = mq0 * d + p
                ks = jj * 128 * d + p
                col = u * 256 + off
                for hh in range(2):
                    P.op("pe", lambda e, hh=hh, ks=ks, qs=qs, nq=nq, col=col: e.matmul(
                        sps[sb_][:, hh, col:col + nq], lhsT=kp[b][hh][:, ks:ks + 127 * d + 1:d],
                        rhs=qg[b][:, qs:qs + (nq - 1) * d + 1:d], start=True, stop=False),
                         reads=(KP[b][hh], QG[b]), writes=(SPS[sb_],))
                    P.op("pe", lambda e, hh=hh, nq=nq, col=col: e.matmul(
                        sps[sb_][:, hh, col:col + nq], lhsT=identb[:], rhs=mask[:, col:col + nq], start=False, stop=True),
                         reads=(IDB, MK), writes=(SPS[sb_],))
            return sb_

        pending = []

        def stage_b(sb_):
            pb = cnt["p"] % NPB
            cnt["p"] += 1
            P.op("act", lambda e: e.activation(out=ptb[pb][:], in_=sps[sb_][:], func=AF.Exp, scale=0.125),
                 reads=(SPS[sb_],), writes=(PTB[pb][0], PTB[pb][1]))
            return pb

        def stage_cd(b, g, batch, pb, zero_acc, hp):
            acc, ACC = acc2[hp % 2], ACC2[hp % 2]
            d = DIL[g]
            L = S // d
            nj = L // 128
            p, us = batch
            while pending:
                pending.pop(0)()
            if zero_acc:
                P.op("pool", lambda e: e.memset(acc[:], 0.0), writes=(ACC,))
            zb = cnt["z"] % 2
            cnt["z"] += 1
            base = 128 * us[0][3] - 64
            for u, (mq0, nq, off, jj) in enumerate(us):
                ci = p * nj + jj
                col = u * 256 + off
                oc_ = mq0 - base
                for hh in range(2):
                    P.op("pe", lambda e, hh=hh, ci=ci, col=col, nq=nq, oc_=oc_, u=u: e.matmul(
                        pz[zb][:, hh, oc_:oc_ + nq], lhsT=vp[b][hh][:, ci, :], rhs=ptb[pb][:, hh, col:col + nq],
                        start=(u == 0), stop=(u == len(us) - 1), skip_group_check=True),
                         reads=(VP[b][hh], PTB[pb][hh]), writes=(PZ[zb],))
            lo = us[0][0]
            hi = us[-1][0] + us[-1][1]
            span = hi - lo
            c0 = lo - base

            def do_acc():
                qs = lo * d + p
                av = acc[:, :, qs:qs + (span - 1) * d + 1:d]
                zv = pz[zb][:, :, c0:c0 + span]
                P.op("dve", lambda e: e.tensor_tensor(out=av, in0=av, in1=zv, op=ALU.add), reads=(PZ[zb], ACC), writes=(ACC,))
            pending.append(do_acc)

        def finish_a(hp):
            acc, ACC = acc2[hp % 2], ACC2[hp % 2]
            while pending:
                pending.pop(0)()
            P.op("sp", lambda e: e.dma_start(out=lnd[0:64, :], in_=acc[64:128, 0, :]), reads=(ACC,), writes=(LND,), dma=True)
            P.op("sp", lambda e: e.dma_start(out=lnd[64:128, :], in_=acc[0:64, 1, :]), reads=(ACC,), writes=(LND,), dma=True)

        def finish_b(hp):
            acc, ACC = acc2[hp % 2], ACC2[hp % 2]
            P.op("act", lambda e: e.activation(out=lnd[:], in_=lnd[:], func=AF.Ln), reads=(LND,), writes=(LND,))
            P.op("act", lambda e: e.activation(out=lnd[:], in_=lnd[:], func=AF.Exp, scale=-1.0), reads=(LND,), writes=(LND,))
            P.op("dve", lambda e: e.tensor_tensor(out=outb[0:64, :], in0=acc[0:64, 0, :], in1=lnd[0:64, :], op=ALU.mult), reads=(ACC, LND), writes=(OUTB,))
            P.op("dve", lambda e: e.tensor_tensor(out=outb[64:128, :], in0=acc[64:128, 1, :], in1=lnd[64:128, :], op=ALU.mult), reads=(ACC, LND), writes=(OUTB,))
            P.op("pool", lambda e: e.dma_start(out=attnT[hp], in_=outb[:]), reads=(OUTB,), writes=(ATT,), dma=True)

        flat = []
        for n, (hp, g) in enumerate(seq):
            bl = units_for(g)
            for i, batch in enumerate(bl):
                flat.append((n, hp, g, n % 2, batch, i == 0, i == len(bl) - 1, i == max(0, len(bl) - 4)))
        load(seq[0][0], seq[0][1], 0)
        load_v(0)
        load_v(1)
        build_vp(0)
        sbs = {}
        NF = len(flat)
        fin_at = {}
        for i in range(NF + 2):
            if i < NF:
                (n, hp, g, b, batch, first, lastb, late) = flat[i]
                sbs[i] = s_mm(b, g, batch)
            if 0 <= i - 1 < NF:
                sbs[i - 1] = stage_b(sbs[i - 1])
            if 0 <= i - 2 < NF:
                (n, hp, g, b, batch, first, lastb, late) = flat[i - 2]
                if first and n + 1 < len(seq):
                    load(seq[n + 1][0], seq[n + 1][1], 1 - b)
                if first and n + 2 < len(seq):
                    load_v(n + 2)
                if late and n + 1 < len(seq):
                    build_vp(n + 1)
                stage_cd(b, g, batch, sbs.pop(i - 2), first and g == 0, hp)
                if lastb and g == 2:
                    finish_a(hp)
                    fin_at[i + 8] = hp
            if i in fin_at:
                finish_b(fin_at.pop(i))
        for k in sorted(fin_at):
            finish_b(fin_at[k])
        P.barrier()
        P.flush()


def build(stop_after=None):
    nc = bass.Bass("TRN2", target_bir_lowering=False)
    C = Ctx()
    declare_io(nc, C)
    with ExitStack() as st:
        P = Prog(nc, st)
        XIN = [Buf("xin%d" % i) for i in range(NT)]
        XS = [Buf("xs%d" % i) for i in range(NT)]
        XO = [Buf("xo%d" % i) for i in range(NT)]
        wb = {}

        class BG:
            def __init__(self):
                self.todo = []

            def add(self, name, src, rows, cols, c0=0):
                dst = nc.dram_tensor("wb_" + name, [rows, cols], BF16).ap()
                wb[name] = dst
                for r0 in range(0, rows, 256):
                    r1 = min(rows, r0 + 256)
                    self.todo.append((name, lambda e, r0=r0, r1=r1, dst=dst, src=src, c0=c0, cols=cols: e.dma_start(
                        out=dst[r0:r1, :], in_=src[r0:r1, c0:c0 + cols], max_dma_last_dim=4096)))

            def tick(self, n):
                for _ in range(n):
                    if not self.todo:
                        return
                    name, fn = self.todo.pop(0)
                    P.op("pool", fn, dma=True)

            def need(self, names):
                while any(nm in names for nm, _ in self.todo):
                    self.tick(1)

        bg = BG()

        def scratch(name, shape, dt=BF16):
            if stop_after is not None and stop_after.startswith("dbg"):
                return nc.dram_tensor(name, list(shape), dt, kind="ExternalOutput").ap()
            return nc.dram_tensor(name, list(shape), dt).ap()

        bg.add("g1_0", C.w1_gate[0], D, DFF)
        bg.add("u1_0", C.w1_up[0], D, DFF)
        bg.add("d1_0", C.w1_down[0], DFF, D)
        for n in range(3):
            bg.add("a_qkv%d" % n, C.a_w_qkv[0], D, D, c0=n * D)
        bg.add("a_o", C.a_w_o[0], D, D)
        bg.add("g2_0", C.w2_gate[0], D, DFF)
        bg.add("u2_0", C.w2_up[0], D, DFF)
        bg.add("d2_0", C.w2_down[0], DFF, D)
        bg.add("g1_1", C.w1_gate[1], D, DFF)
        bg.add("u1_1", C.w1_up[1], D, DFF)
        bg.add("d1_1", C.w1_down[1], DFF, D)
        for n in range(7):
            bg.add("b_in%d" % n, C.b_w_in[0], D, D, c0=n * D)
        bg.add("b_o", C.b_w_o[0], D, D)
        bg.add("g2_1", C.w2_gate[1], D, DFF)
        bg.add("u2_1", C.w2_up[1], D, DFF)
        bg.add("d2_1", C.w2_down[1], DFF, D)
        bg.need(("g1_0", "u1_0", "d1_0"))
        P.barrier()
        P.flush()

        qk = [scratch("qk%d" % n, (8, 128, S)) for n in range(6)]
        QK = [Buf("qk%d" % n) for n in range(6)]
        vs0 = scratch("vs0", (8, 128, 32, 128))
        vn1 = scratch("vn1", (S, D))
        VS = Buf("vs")
        attnT = scratch("attnT", (8, 128, S))
        ATT = Buf("attnT")
        xs = C.xs

        def done(tag, src):
            return stop_after == tag

        last = (stop_after == "ffn1_0")
        P.pre_barrier = lambda: bg.need(("a_qkv0", "a_qkv1", "a_qkv2"))
        ffn_phase(P, nc, C, C.x, XIN, C.out if last else xs, XO if last else XS, C.ln_ffn1[0:1, :], wb["g1_0"], wb["u1_0"], wb["d1_0"], bg=bg)
        if last:
            return nc
        P.pre_barrier = None
        qkv_phase(P, nc, C, xs, XS, C.ln_mix[0:1, :], [(wb["a_qkv0"], qk[0], QK[0]), (wb["a_qkv1"], qk[1], QK[1])],
                  (wb["a_qkv2"], vs0, VS, "hpce"), bg=bg)
        if stop_after == "dbg_qkv0":
            return nc
        P.pre_barrier = lambda: bg.need(("a_o",))
        att0_phase(P, nc, C, qk[0], qk[1], vs0, attnT, ATT, C.a_lambda[0], 0.8 - 0.6 * float(np.exp(-0.3 * 0)), bg=bg)
        if stop_after == "dbg_att0":
            return nc
        P.pre_barrier = lambda: bg.need(("g2_0", "u2_0", "d2_0"))
        oproj_phase(P, nc, C, xs, XS, attnT, wb["a_o"], subln=(C.a_subln[0:1, :], 1.0 - (0.8 - 0.6 * float(np.exp(-0.3 * 0)))), bg=bg)
        if stop_after == "dbg_oproj0":
            return nc
        last = (stop_after == "layer0")
        P.pre_barrier = lambda: bg.need(("g1_1", "u1_1", "d1_1"))
        ffn_phase(P, nc, C, xs, XS, C.out if last else xs, XO if last else XS, C.ln_ffn2[0:1, :], wb["g2_0"], wb["u2_0"], wb["d2_0"], bg=bg)
        if last:
            return nc
        P.pre_barrier = lambda: bg.need(tuple("b_in%d" % n for n in range(7)))
        ffn_phase(P, nc, C, xs, XS, xs, XS, C.ln_ffn1[1:2, :], wb["g1_1"], wb["u1_1"], wb["d1_1"], bg=bg)
        P.pre_barrier = None
        qkv_phase(P, nc, C, xs, XS, C.ln_mix[1:2, :], [(wb["b_in%d" % n], qk[n], QK[n]) for n in range(6)],
                  (wb["b_in6"], vn1, VS, "nat"))
        if stop_after == "dbg_qkv1":
            return nc
        P.pre_barrier = lambda: bg.need(("b_o",))
        att1_phase(P, nc, C, qk, vn1, attnT, ATT)
        if stop_after == "dbg_att1":
            return nc
        P.pre_barrier = lambda: bg.need(("g2_1", "u2_1", "d2_1"))
        oproj_phase(P, nc, C, xs, XS, attnT, wb["b_o"], subln=None, bg=bg)
        P.pre_barrier = None
        ffn_phase(P, nc, C, xs, XS, C.out, XO, C.ln_ffn2[1:2, :], wb["g2_1"], wb["u2_1"], wb["d2_1"], final_gain=C.ln_final[0:1, :])
    return nc


_CONSTS = None


def consts():
    global _CONSTS
    if _CONSTS is None:
        import ml_dtypes
        c = {}
        c["c_ident"] = np.eye(128, dtype=np.float32).astype(ml_dtypes.bfloat16)
        inv = (10000.0 ** (-np.arange(0, 64, 2, dtype=np.float32) / 64.0)).astype(np.float32)
        ang = np.arange(S, dtype=np.float32)[:, None] * inv[None, :]
        cos, sin = np.cos(ang).astype(np.float32), np.sin(ang).astype(np.float32)
        c["c_cs"] = np.ascontiguousarray(np.concatenate([cos, cos, -sin, sin], axis=1).astype(np.float32))
        a = np.arange(128)
        ma = (a[:, None] >= a[None, :]).astype(np.float32)
        mb = (a[:, None] <= a[None, :]).astype(np.float32)
        c["c_mask"] = np.ascontiguousarray((np.concatenate([mb, ma, mb, ma], axis=1) - 1.0) * 30000.0).astype(ml_dtypes.bfloat16)
        _CONSTS = c
    return _CONSTS


def make_in_maps(inputs):
    x = np.ascontiguousarray(np.asarray(inputs["x"], dtype=np.float32))
    shared = {}
    for k, v in inputs.items():
        if k == "x":
            continue
        a = np.ascontiguousarray(np.asarray(v, dtype=np.float32))
        if k == "ln_final":
            a = a.reshape(1, D)
        shared[k] = a
    shared.update(consts())
    in_maps = []
    for c in range(x.shape[0]):
        m = dict(shared)
        m["x"] = x[c]
        in_maps.append(m)
    return in_maps


def kernel(**inputs):
    nc = build()
    in_maps = make_in_maps(inputs)
    res = run_bass_kernel_spmd(nc, in_maps, core_ids=list(range(NCORES)))
    return np.stack([np.asarray(r["out"]) for r in res.results], axis=0).astype(np.float32)
```
